# Optimizing a Trainium2 kernel written in Bass

```python
import jax, jax.numpy as jnp
from jax import lax
import numpy as np

D_MODEL = 1024
BATCH = 32
SEQ = 2048
DEPTH = 1
DEC_BATCH = 2
DEC_SEQ = 16384
PAST_LEN = 128

N_Q_HEADS = 8
N_KV_HEADS = 2
HEAD_DIM = 64
ATTN_WIDTH = N_Q_HEADS * HEAD_DIM
KV_WIDTH = N_KV_HEADS * HEAD_DIM
WINDOW = 128
BLOCK = 128
ROPE_THETA = 10000.0
LRU_WIDTH = 512
LRU_BLOCKS = 8
LRU_BLOCK_DIM = LRU_WIDTH // LRU_BLOCKS
LRU_C = 8.0
CONV_WIDTH = 4
CONV_LEFT = 2
MIX_WIDTH = ATTN_WIDTH + LRU_WIDTH
IN_WIDTH = ATTN_WIDTH + 2 * KV_WIDTH + 2 * LRU_WIDTH
PEER_HEADS = 8
PEER_NKEYS = 128
PEER_EXPERTS = PEER_NKEYS * PEER_NKEYS
PEER_DKEY = 256
PEER_HALF = PEER_DKEY // 2
PEER_TOPK = 16
PEER_CHUNK = 128
PLE_DIM = 256
EPS = 1e-6

kernel_name = 'hymba_swa_rglru_peer_encoder'


def rms_norm(x, g):
    xf = x.astype(jnp.float32)
    y = xf * lax.rsqrt(jnp.mean(xf * xf, axis=-1, keepdims=True) + EPS)
    return (y * g.astype(jnp.float32)).astype(x.dtype)


def apply_rope(t):
    seq = t.shape[1]
    half = HEAD_DIM // 2
    inv_freq = ROPE_THETA ** (-jnp.arange(half, dtype=jnp.float32) / half)
    ang = jnp.arange(seq, dtype=jnp.float32)[:, None] * inv_freq[None, :]
    cos = jnp.cos(ang)[None, :, None, :]
    sin = jnp.sin(ang)[None, :, None, :]
    tf = t.astype(jnp.float32)
    t1, t2 = tf[..., :half], tf[..., half:]
    return jnp.concatenate([t1 * cos - t2 * sin, t2 * cos + t1 * sin], axis=-1).astype(t.dtype)


def banded_window_attention(q, k, v, sink):
    b, s = q.shape[0], q.shape[1]
    nb = s // BLOCK
    grp = N_Q_HEADS // N_KV_HEADS
    qb = q.astype(jnp.float32).reshape(b, nb, BLOCK, N_KV_HEADS, grp, HEAD_DIM)

    def bands(t):
        tp = jnp.pad(t.astype(jnp.float32), ((0, 0), (BLOCK, BLOCK), (0, 0), (0, 0)))
        tp = tp.reshape(b, nb + 2, BLOCK, N_KV_HEADS, HEAD_DIM)
        return jnp.concatenate([tp[:, :-2], tp[:, 1:-1], tp[:, 2:]], axis=2)

    kb = bands(k)
    vb = bands(v)
    scores = jnp.einsum('bnqkgd,bnckd->bnkgqc', qb, kb) * (HEAD_DIM ** -0.5)
    qpos = jnp.arange(nb)[:, None] * BLOCK + jnp.arange(BLOCK)[None, :]
    kpos = (jnp.arange(nb)[:, None] - 1) * BLOCK + jnp.arange(3 * BLOCK)[None, :]
    valid = ((jnp.abs(qpos[:, :, None] - kpos[:, None, :]) <= WINDOW)
             & (kpos[:, None, :] >= 0) & (kpos[:, None, :] < s))
    scores = jnp.where(valid[None, :, None, None], scores, -jnp.inf)
    sink_l = sink.astype(jnp.float32).reshape(N_KV_HEADS, grp)[None, None, :, :, None, None]
    m = jnp.maximum(jnp.max(scores, axis=-1, keepdims=True), sink_l)
    e = jnp.exp(scores - m)
    denom = jnp.sum(e, axis=-1, keepdims=True) + jnp.exp(sink_l - m)
    out = jnp.einsum('bnkgqc,bnckd->bnqkgd', e / denom, vb)
    return out.reshape(b, s, ATTN_WIDTH)


def centred_dwconv(x, w, bias):
    y = lax.conv_general_dilated(
        x, w[:, None, :].astype(x.dtype), window_strides=(1,),
        padding=[(CONV_LEFT, CONV_WIDTH - 1 - CONV_LEFT)],
        dimension_numbers=('NWC', 'WIO', 'NWC'), feature_group_count=LRU_WIDTH)
    return y + bias.astype(x.dtype)


def linear_combine(left, right):
    a_l, h_l = left
    a_r, h_r = right
    return a_l * a_r, a_r * h_l + h_r


def rglru_scan(xc, wa, ba, wx, bx, lam):
    b, s, _ = xc.shape
    xf = xc.astype(jnp.float32)
    xb = xf.reshape(b, s, LRU_BLOCKS, LRU_BLOCK_DIM)
    gate_r = jax.nn.sigmoid(jnp.einsum('bshi,hij->bshj', xb, wa.astype(jnp.float32)).reshape(b, s, LRU_WIDTH)
                            + ba.astype(jnp.float32))
    gate_i = jax.nn.sigmoid(jnp.einsum('bshi,hij->bshj', xb, wx.astype(jnp.float32)).reshape(b, s, LRU_WIDTH)
                            + bx.astype(jnp.float32))
    log_a = LRU_C * gate_r * jax.nn.log_sigmoid(lam.astype(jnp.float32))
    a = jnp.exp(log_a)
    u = jnp.sqrt(-jnp.expm1(2.0 * log_a)) * (gate_i * xf)
    _, h = lax.associative_scan(linear_combine, (a, u), axis=1)
    return h


def peer_ffn(x, wq, keys, u_tab, v_tab):
    b, s, d = x.shape
    xt = x.reshape(-1, PEER_CHUNK, d)

    def retrieve(xc):
        q = (xc @ wq).astype(jnp.float32).reshape(PEER_CHUNK, PEER_HEADS, 2, PEER_HALF)
        sub = jnp.einsum('thpd,hpkd->thpk', q, keys.astype(jnp.float32))
        sub_s, sub_i = lax.top_k(sub, PEER_TOPK)
        cand_s = (sub_s[:, :, 0, :, None] + sub_s[:, :, 1, None, :]).reshape(PEER_CHUNK, PEER_HEADS, -1)
        cand_i = (sub_i[:, :, 0, :, None] * PEER_NKEYS + sub_i[:, :, 1, None, :]).reshape(PEER_CHUNK, PEER_HEADS, -1)
        best_s, pos = lax.top_k(cand_s, PEER_TOPK)
        idx = jnp.take_along_axis(cand_i, pos, axis=-1)
        g = jax.nn.softmax(best_s, axis=-1)
        u_e = jnp.take(u_tab, idx, axis=0).astype(jnp.float32)
        act = jax.nn.gelu(jnp.einsum('td,thkd->thk', xc.astype(jnp.float32), u_e))
        v_e = jnp.take(v_tab, idx, axis=0).astype(jnp.float32)
        return jnp.einsum('thk,thkd->td', g * act, v_e).astype(xc.dtype)

    return lax.map(retrieve, xt).reshape(b, s, d)


def setup_inputs(seed: int = 0) -> dict:
    key = jax.random.key(seed)
    ks = jax.random.split(key, 26)
    nrm = jax.random.normal
    f32 = jnp.float32
    uu = jax.random.uniform(ks[13], (DEPTH, 2, LRU_WIDTH), f32, minval=0.9, maxval=0.999)
    a0 = uu ** (1.0 / LRU_C)
    lru_lambda = jnp.log(a0) - jnp.log1p(-a0)
    return {
        'x_prompt': nrm(ks[0], (BATCH, SEQ, D_MODEL), f32),
        'x_sample': nrm(ks[1], (DEC_BATCH, DEC_SEQ, D_MODEL), f32),
        'p_prompt': nrm(ks[2], (DEPTH, BATCH, SEQ, PLE_DIM), f32),
        'p_sample': nrm(ks[3], (DEPTH, DEC_BATCH, DEC_SEQ, PLE_DIM), f32),
        'mix_norm_g': 1.0 + 0.05 * nrm(ks[4], (DEPTH, D_MODEL), f32),
        'w_in': nrm(ks[5], (DEPTH, D_MODEL, IN_WIDTH), f32) * D_MODEL ** -0.5,
        'attn_sink': 0.5 * nrm(ks[6], (DEPTH, N_Q_HEADS), f32),
        'conv_w': 0.5 * nrm(ks[7], (DEPTH, CONV_WIDTH, LRU_WIDTH), f32),
        'conv_b': 0.01 * nrm(ks[8], (DEPTH, LRU_WIDTH), f32),
        'lru_wa': nrm(ks[9], (DEPTH, 2, LRU_BLOCKS, LRU_BLOCK_DIM, LRU_BLOCK_DIM), f32) * LRU_BLOCK_DIM ** -0.5,
        'lru_ba': 0.01 * nrm(ks[10], (DEPTH, 2, LRU_WIDTH), f32),
        'lru_wx': nrm(ks[11], (DEPTH, 2, LRU_BLOCKS, LRU_BLOCK_DIM, LRU_BLOCK_DIM), f32) * LRU_BLOCK_DIM ** -0.5,
        'lru_bx': 0.01 * nrm(ks[12], (DEPTH, 2, LRU_WIDTH), f32),
        'lru_lambda': lru_lambda,
        'attn_out_norm_g': 1.0 + 0.05 * nrm(ks[14], (DEPTH, ATTN_WIDTH), f32),
        'lru_out_norm_g': 1.0 + 0.05 * nrm(ks[15], (DEPTH, LRU_WIDTH), f32),
        'w_out': nrm(ks[16], (DEPTH, MIX_WIDTH, D_MODEL), f32) * MIX_WIDTH ** -0.5,
        'ffn_norm_g': 1.0 + 0.05 * nrm(ks[17], (DEPTH, D_MODEL), f32),
        'peer_wq': nrm(ks[18], (DEPTH, D_MODEL, PEER_HEADS * PEER_DKEY), f32) * D_MODEL ** -0.5,
        'peer_keys': nrm(ks[19], (DEPTH, PEER_HEADS, 2, PEER_NKEYS, PEER_HALF), f32) * PEER_HALF ** -0.5,
        'peer_u': nrm(ks[20], (DEPTH, PEER_EXPERTS, D_MODEL), f32) * D_MODEL ** -0.5,
        'peer_v': 0.3 * nrm(ks[21], (DEPTH, PEER_EXPERTS, D_MODEL), f32),
        'ple_norm_g': 1.0 + 0.05 * nrm(ks[22], (DEPTH, D_MODEL), f32),
        'ple_w_gate': nrm(ks[23], (DEPTH, D_MODEL, D_MODEL), f32) * D_MODEL ** -0.5,
        'ple_w_proj': nrm(ks[24], (DEPTH, PLE_DIM, D_MODEL), f32) * PLE_DIM ** -0.5,
        'final_norm_g': 1.0 + 0.05 * nrm(ks[25], (D_MODEL,), f32),
    }


def reference(x_prompt, x_sample, p_prompt, p_sample, mix_norm_g, w_in, attn_sink, conv_w, conv_b,
              lru_wa, lru_ba, lru_wx, lru_bx, lru_lambda, attn_out_norm_g, lru_out_norm_g, w_out,
              ffn_norm_g, peer_wq, peer_keys, peer_u, peer_v, ple_norm_g, ple_w_gate, ple_w_proj,
              final_norm_g):
    o1 = ATTN_WIDTH
    o2 = o1 + KV_WIDTH
    o3 = o2 + KV_WIDTH
    o4 = o3 + LRU_WIDTH

    def encoder(x, p):
        h = x
        b, s, _ = x.shape
        for l in range(DEPTH):
            xn = rms_norm(h, mix_norm_g[l])
            z = xn @ w_in[l]
            q, k, v, xr, gr = jnp.split(z, [o1, o2, o3, o4], axis=-1)
            q = apply_rope(q.reshape(b, s, N_Q_HEADS, HEAD_DIM))
            k = apply_rope(k.reshape(b, s, N_KV_HEADS, HEAD_DIM))
            v = v.reshape(b, s, N_KV_HEADS, HEAD_DIM)
            attn = banded_window_attention(q, k, v, attn_sink[l]).astype(h.dtype)
            xc = centred_dwconv(xr, conv_w[l], conv_b[l])
            h_fwd = rglru_scan(xc, lru_wa[l, 0], lru_ba[l, 0], lru_wx[l, 0], lru_bx[l, 0], lru_lambda[l, 0])
            h_bwd = jnp.flip(rglru_scan(jnp.flip(xc, axis=1), lru_wa[l, 1], lru_ba[l, 1], lru_wx[l, 1],
                                        lru_bx[l, 1], lru_lambda[l, 1]), axis=1)
            lru = (jax.nn.gelu(gr.astype(jnp.float32)) * (h_fwd + h_bwd)).astype(h.dtype)
            merged = jnp.concatenate([rms_norm(attn, attn_out_norm_g[l]), rms_norm(lru, lru_out_norm_g[l])], axis=-1)
            h = h + merged @ w_out[l]
            h = h + peer_ffn(rms_norm(h, ffn_norm_g[l]), peer_wq[l], peer_keys[l], peer_u[l], peer_v[l])
            gate = jax.nn.sigmoid(rms_norm(h, ple_norm_g[l]) @ ple_w_gate[l])
            h = h + gate * (p[l] @ ple_w_proj[l])
        return rms_norm(h, final_norm_g)

    y_prompt = encoder(x_prompt, p_prompt)
    y_sample = encoder(x_sample, p_sample)
    return (y_prompt, y_sample)
```

```python
from contextlib import ExitStack
import numpy as np
import ml_dtypes
import concourse.bass as bass
import concourse.mybir as mybir
from concourse.bass_utils import run_bass_kernel_spmd

F32 = mybir.dt.float32
BF16 = mybir.dt.bfloat16
I32 = mybir.dt.int32
U32 = mybir.dt.uint32
AF = mybir.ActivationFunctionType
ALU = mybir.AluOpType
AX = mybir.AxisListType

D = 1024
NCORES = 8
EPS = 1e-6
NQKV = 1408
NLRU = 1024


class Buf:
    __slots__ = ("name", "w", "r")

    def __init__(self, name):
        self.name = name
        self.w = None
        self.r = []


class Op:
    __slots__ = ("eng", "fn", "deps", "dma", "sig", "needed", "idx")

    def __init__(self, eng, fn, dma):
        self.eng = eng
        self.fn = fn
        self.deps = []
        self.dma = dma
        self.sig = None
        self.needed = False


class Sched:
    ENGS = ("pe", "act", "dve", "pool", "sp")

    def __init__(self):
        self.ops = []
        self.bufs = []
        self.bar = []
        self.last = {}
        self.dma_last = {}

    def buf(self, name):
        b = Buf(name)
        self.bufs.append(b)
        return b

    def bufs_n(self, name, n):
        return [self.buf(f"{name}{i}") for i in range(n)]

    def op(self, eng, fn, reads=(), writes=(), dma=None):
        o = Op(eng, fn, dma)
        deps = list(self.bar)
        for b in reads:
            if b.w is not None:
                deps.append(b.w)
        for b in writes:
            if b.w is not None:
                deps.append(b.w)
            deps.extend(b.r)
        o.deps = deps
        for b in reads:
            b.r.append(o)
        for b in writes:
            b.w = o
            b.r = []
        self.ops.append(o)
        self.last[eng] = o
        if dma is not None:
            self.dma_last[dma] = o
        return o

    def barrier(self):
        self.bar = list(self.last.values()) + list(self.dma_last.values())
        for b in self.bufs:
            b.w = None
            b.r = []

    def emit(self, nc, stack, final_eng="sp"):
        fin = Op(final_eng, None, None)
        fin.deps = list(self.last.values()) + list(self.dma_last.values())
        self.ops.append(fin)
        for o in self.ops:
            for d in o.deps:
                d.needed = True
        MAXC = 30000
        eng_sems = {e: [] for e in self.ENGS}
        eng_cnt = {e: MAXC for e in self.ENGS}
        dma_sems = {}
        dma_cnt = {}
        nsem = [0]

        def newsem(nm):
            nsem[0] += 1
            return stack.enter_context(nc.semaphore(f"{nm}_{nsem[0]}"))

        for o in self.ops:
            if o.dma is not None:
                if o.dma not in dma_sems or dma_cnt[o.dma] >= 32000:
                    dma_sems[o.dma] = newsem("d")
                    dma_cnt[o.dma] = 0
                dma_cnt[o.dma] += 16
                o.sig = (dma_sems[o.dma], dma_cnt[o.dma])
            elif o.needed:
                if eng_cnt[o.eng] >= MAXC:
                    eng_sems[o.eng].append(newsem(o.eng))
                    eng_cnt[o.eng] = 0
                eng_cnt[o.eng] += 1
                o.sig = (eng_sems[o.eng][-1], eng_cnt[o.eng])
        per_eng = {e: [] for e in self.ENGS}
        for o in self.ops:
            per_eng[o.eng].append(o)
        self.n_ops = len(self.ops)

        def run(eng_name, eng):
            waited = {}
            for o in per_eng[eng_name]:
                for d in o.deps:
                    if d.dma is None and d.eng == "pe" and eng_name == "pe":
                        continue
                    sem, val = d.sig
                    k = id(sem)
                    if waited.get(k, 0) >= val:
                        continue
                    waited[k] = val
                    eng.wait_ge(sem, val)
                if o.fn is None:
                    continue
                inst = o.fn(eng)
                if o.dma is not None:
                    inst.then_inc(o.sig[0], 16)
                elif o.sig is not None:
                    inst.then_inc(o.sig[0], 1)

        with nc.Block() as block:
            @block.sync
            def _(e):
                run("sp", e)

            @block.scalar
            def _(e):
                run("act", e)

            @block.vector
            def _(e):
                run("dve", e)

            @block.gpsimd
            def _(e):
                run("pool", e)

            @block.tensor
            def _(e):
                run("pe", e)


def apx(base, dims, off=0):
    return bass.AP(tensor=base.tensor, offset=base.offset + off, ap=[list(base.ap[0])] + [list(d) for d in dims])


def build_program(T):
    NT = T + 2
    S = T * 128
    SE = NT * 128
    NFULL = 6
    NLIGHT = 7
    NPIECE = (S + 511) // 512
    PW = min(512, S)
    nc = bass.Bass("TRN2", target_bir_lowering=False)
    dt_in = lambda n, s, d=F32: nc.dram_tensor(n, s, d, kind="ExternalInput").ap()
    x_full = dt_in("x_full", [NFULL, SE, D])
    x_light = dt_in("x_light", [NLIGHT, SE, D])
    p_in = dt_in("p_in", [NFULL, S, 256])
    w_qkv = dt_in("w_qkv", [D, NQKV])
    w_lru = dt_in("w_lru", [D, NLRU])
    g_cols = dt_in("g_cols", [128, 4, 8])
    g_rows = dt_in("g_rows", [2, D])
    og_cols = dt_in("og_cols", [128, 8])
    sink = dt_in("sink", [1, 8])
    gsets = dt_in("gsets", [2 + NLIGHT, 128, 8 * 128 + 40])
    w_out = dt_in("w_out", [D, D])
    w_pq = dt_in("w_pq", [D, 2048])
    keysT = dt_in("keysT", [128, 16, 128])
    peer_u = dt_in("peer_u", [16384, D])
    peer_v = dt_in("peer_v", [16384, D])
    w_gate = dt_in("w_gate", [D, D])
    w_proj = dt_in("w_proj", [256, D])
    rope = dt_in("rope", [NFULL, 2, 128, SE])
    masks = dt_in("masks", [2 + 2 * NFULL, 128, 128], BF16)
    ident_in = dt_in("ident", [128, 128], BF16)
    cfg_in = dt_in("cfg", [128, 32])
    iota_in = dt_in("iota16", [128, 16])
    y_out = nc.dram_tensor("y_out", [NFULL, S, D], F32, kind="ExternalOutput").ap()
    uvb = nc.dram_tensor("uvb", [16384, 2 * D], BF16, kind="Internal").ap()

    sc = Sched()
    stack = ExitStack()
    with stack:
        sb = lambda n, s, d=F32: stack.enter_context(nc.sbuf_tensor(n, s, d))
        ident = sb("ident_sb", [128, 128], BF16)
        woutb = sb("woutb", [128, 8, D], BF16)
        lruT = sb("lruT", [128, 4, S], BF16)
        attnT = sb("attnT", [128, 4, S], BF16)
        grow = sb("grow", [128, 2, D])
        gcol = sb("gcol", [128, 4, 8])
        ogcol = sb("ogcol", [128, 8])
        esink = sb("esink", [128, 8])
        mstd = sb("mstd", [128, 2, 128], BF16)
        cfg = sb("cfg_sb", [128, 32])
        iota16 = sb("iota_sb", [128, 16])
        gset01 = sb("gset01", [128, 2, 8 * 128], BF16)
        gvec01 = sb("gvec01", [128, 2, 40])
        clam01 = sb("clam01", [128, 2, 4])
        rstdl = sb("rstdl", [128, T])
        ssl = sb("ssl", [128, T])
        ones_f = sb("ones_f", [128, 1])
        carries = sb("carries", [128, 16, 4])
        small = sb("small", [128, 64])
        stage = sb("stage", [128, 2048])
        RB = 70400
        R = sb("R", [128, RB], BF16)
        cur = [0]

        def carve(shape, dt):
            n = int(np.prod(shape))
            units = n * (2 if dt in (F32, I32, U32) else 1)
            a = R[:, cur[0]:cur[0] + units]
            cur[0] += units + (units % 2)
            assert cur[0] <= RB, (cur[0], RB)
            if dt != BF16:
                a = a.bitcast(dt)
            if len(shape) == 2:
                return a.rearrange("p (a b) -> p a b", b=shape[1])
            if len(shape) == 3:
                return a.rearrange("p (a b c) -> p a b c", b=shape[1], c=shape[2])
            return a

        xnT = carve([8, SE], BF16)
        markX = cur[0]
        wqkvb = carve([8, NQKV], BF16)
        cosT = carve([SE], F32)
        sinT = carve([SE], F32)
        qrT = carve([4, S], BF16)
        krT = carve([SE], BF16)
        vaug = carve([NT, 130], BF16)
        Et = [carve([512], BF16) for _ in range(3)]
        ropet = [carve([512], F32) for _ in range(2)]
        attnf = carve([512], F32)
        attnb = carve([512], BF16)
        bmask = carve([2, 128], BF16)
        xin = [carve([D], F32) for _ in range(2)]
        xnb = [carve([D], BF16) for _ in range(2)]
        endA2 = cur[0]
        cur[0] = markX
        wlrub = carve([8, NLRU], BF16)
        xr_c = carve([SE], F32)
        xc_c = carve([S], F32)
        xcb_c = carve([S], BF16)
        ra = carve([S], F32)
        iu = carve([S], F32)
        sh = carve([S], F32)
        acc4 = carve([S], F32)
        gg = carve([S], F32)
        gsetl = carve([8 * 128], BF16)
        gvecl = carve([40], F32)
        claml = carve([4], F32)
        xinL = [carve([D], F32) for _ in range(2)]
        xnbL = [carve([D], BF16) for _ in range(2)]
        endLRU = cur[0]
        cur[0] = 0
        wpqb = carve([8, 2048], BF16)
        keysb = carve([16, 128], BF16)
        wgb = carve([8, D], BF16)
        wpb = carve([2, D], BF16)
        xB = carve([D], F32)
        h1s = [carve([D], F32) for _ in range(2)]
        xn2b = carve([D], BF16)
        xn2T = carve([8, 128], BF16)
        qpT = carve([16, 128], BF16)
        subs = carve([16, 128], F32)
        subt = carve([128], F32)
        uvbuf = [carve([2 * D], BF16) for _ in range(6)]
        xn2gbs = [carve([D], BF16) for _ in range(2)]
        junkL = carve([D], BF16)
        diag = [carve([128], BF16) for _ in range(4)]
        junk = carve([D], BF16)
        pBs = [carve([256], F32) for _ in range(2)]
        pBb = carve([256], BF16)
        pT = carve([2, 128], BF16)
        gate = carve([D], F32)
        top0 = carve([16, 16], F32)
        idxu = carve([16, 16], U32)
        idxf = carve([16, 16], F32)
        cand = subs.rearrange("p a b -> p (a b)").rearrange("p (h c) -> p h c", c=256)
        candt = carve([256], F32)
        cbest = carve([8, 16], F32)
        posu = carve([8, 16], U32)
        posf = carve([8, 16], F32)
        afl = carve([8, 16], F32)
        bfl = carve([8, 16], F32)
        oh = subs.rearrange("p a b -> p (a b)").rearrange("p (k a) -> p k a", a=16)
        isel = carve([128], F32)
        jsel = carve([128], F32)
        eidf = carve([128], F32)
        eidis = [carve([128], I32) for _ in range(2)]
        gsms = [carve([8, 16], F32) for _ in range(2)]
        dots = carve([128], F32)
        wts = carve([128], F32)
        endB = cur[0]

        pst = stack.enter_context(nc.psum_tensor("pst", [128, 1024], BF16))
        bank = [stack.enter_context(nc.psum_tensor(f"bank{i}", [128, 512], F32)) for i in range(7)]

        B = sc.buf
        b_pst = B("pst")
        b_bank = sc.bufs_n("bank", 7)
        b_stage = B("stage")
        b_wout = B("wout"); b_lruT = [B(f"lruT{c}") for c in range(4)]; b_attnT = [B(f"attnT{j}") for j in range(T)]
        b_consts = B("consts")
        b_small = B("small")
        b_carr = B("carries")
        b_rstdl = B("rstdl"); b_ssl = B("ssl")

        def dma(eng, key, out, in_, reads=(), writes=()):
            return sc.op(eng, lambda e, out=out, in_=in_: e.dma_start(out=out, in_=in_), reads, writes, dma=key)

        dma("sp", "c0", ident[:], ident_in[:, :], (), (b_consts,))
        dma("sp", "c0", grow[:, 0, :], g_rows[0:1, :].partition_broadcast(128), (), (b_consts,))
        dma("sp", "c0", grow[:, 1, :], g_rows[1:2, :].partition_broadcast(128), (), (b_consts,))
        dma("sp", "c0", gcol[:], g_cols[:, :, :], (), (b_consts,))
        dma("sp", "c0", ogcol[:], og_cols[:, :], (), (b_consts,))
        dma("sp", "c0", esink[:], sink[0:1, :].partition_broadcast(128), (), (b_consts,))
        dma("sp", "c0", mstd[:], masks[0:2].rearrange("m p q -> p m q"), (), (b_consts,))
        dma("sp", "c0", cfg[:], cfg_in[:, :], (), (b_consts,))
        dma("sp", "c0", iota16[:], iota_in[:, :], (), (b_consts,))
        for d_ in range(2):
            dma("pool", "c1", gset01[:, d_, :], gsets[d_, :, 0:1024], (), (b_consts,))
            dma("sp", "c0", gvec01[:, d_, :], gsets[d_, :, 1024:1064], (), (b_consts,))
        b_c2 = B("consts2")
        sc.op("act", lambda e: e.activation(out=esink[:], in_=esink[:], func=AF.Exp), (b_consts,), (b_c2,))
        sc.op("dve", lambda e: e.memset(ones_f[:], 1.0), (), (b_c2,))
        sc.op("dve", lambda e: e.memset(carries[:], 0.0), (), (b_carr,))

        def clam_compute(dst, lamsrc, rd, wr):
            sc.op("act", lambda e: e.activation(out=dst, in_=lamsrc, func=AF.Exp, scale=-1.0), rd, wr)
            sc.op("act", lambda e: e.activation(out=dst, in_=dst, func=AF.Ln, bias=1.0), wr, wr)
            sc.op("act", lambda e: e.mul(out=dst, in_=dst, mul=-8.0), wr, wr)

        for d_ in range(2):
            clam_compute(clam01[:, d_, :], gvec01[:, d_, 8:12], (b_consts,), (b_c2,))

        def load_w(dst, src, K, N, gain, bw):
            for dk in range(K):
                for n0 in range(0, N, 2048):
                    n1 = min(N, n0 + 2048)
                    dma("sp", "stg", stage[:, 0:n1 - n0], src[dk * 128:(dk + 1) * 128, n0:n1], (), (b_stage,))
                    if gain is None:
                        sc.op("dve", lambda e, dk=dk, n0=n0, n1=n1: e.tensor_copy(out=dst[:, dk, n0:n1], in_=stage[:, 0:n1 - n0]), (b_stage, b_consts), (bw,))
                    else:
                        sc.op("dve", lambda e, dk=dk, n0=n0, n1=n1, gain=gain: e.tensor_scalar(out=dst[:, dk, n0:n1], in0=stage[:, 0:n1 - n0], scalar1=gain(dk), scalar2=None, op0=ALU.mult), (b_stage, b_consts), (bw,))

        load_w(woutb, w_out, 8, D, lambda dk: ogcol[:, dk:dk + 1], b_wout)
        b_uvb = B("uvb")
        uvb_ops = []
        for i in range(64):
            r0, r1 = i * 256, (i + 1) * 256
            uvb_ops.append(dma("pool", "uvc", uvb[r0:r1, 0:D], peer_u[r0:r1, :], (), ()))
            uvb_ops.append(dma("pool", "uvc", uvb[r0:r1, D:2 * D], peer_v[r0:r1, :], (), ()))

        def rms_stats(junk_b, x_ap, n, junk_ap, ss_ap, rstd_ap, rd, wr_small):
            sc.op("act", lambda e: e.activation(out=junk_ap, in_=x_ap, func=AF.Square, accum_out=ss_ap), rd, (wr_small, junk_b))
            sc.op("act", lambda e: e.activation(out=ss_ap, in_=ss_ap, func=AF.Sqrt, scale=1.0 / n, bias=EPS), (wr_small,), (wr_small,))
            sc.op("dve", lambda e: e.reciprocal(out=rstd_ap, in_=ss_ap), (wr_small,), (wr_small,))

        def transposes(src_b, nchunk, rd, evac_out, evac_wr, evac_eng="act"):
            for k in range(nchunk):
                sc.op("pe", lambda e, k=k: e.transpose(pst[:, k * 128:(k + 1) * 128], src_b[:, k * 128:(k + 1) * 128], ident[:]),
                      tuple(rd) + (b_consts,), (b_pst,))
            src = pst[:, 0:nchunk * 128].rearrange("p (a b) -> p a b", b=128)
            if evac_eng == "act":
                sc.op("act", lambda e: e.copy(out=evac_out, in_=src), (b_pst,), evac_wr)
            else:
                sc.op("dve", lambda e: e.tensor_copy(out=evac_out, in_=src), (b_pst,), evac_wr)

        b_xnT = sc.bufs_n("xnT", NT)

        def phase_A1(xsrc, xin_, xnb_):
            b_xin = sc.bufs_n("xin", 2); b_xnb = sc.bufs_n("xnb", 2); b_sm = sc.bufs_n("a1s", 2)
            for j in range(NT):
                s_ = j % 2
                dma("sp", f"xin{s_}", xin_[s_], xsrc[j * 128:(j + 1) * 128, :], (), (b_xin[s_],))
                ss = small[:, 2 * s_:2 * s_ + 1]; rs = small[:, 2 * s_ + 1:2 * s_ + 2]
                rms_stats(b_xnb[s_], xin_[s_], D, xnb_[s_], ss, rs, (b_xin[s_],), b_sm[s_])
                sc.op("dve", lambda e, s_=s_, rs=rs: e.tensor_scalar(out=xnb_[s_], in0=xin_[s_], scalar1=rs, scalar2=None, op0=ALU.mult),
                      (b_xin[s_], b_sm[s_]), (b_xnb[s_],))
                transposes(xnb_[s_], 8, (b_xnb[s_],), xnT[:, :, j * 128:(j + 1) * 128], (b_xnT[j],))

        def proj(wb, col0, c0, c1, bk, rd_extra=()):
            tiles = [b_xnT[j] for j in range(c0 // 128, (c1 + 127) // 128)]
            for dk in range(8):
                sc.op("pe", lambda e, dk=dk: e.matmul(bank[bk][:, 0:c1 - c0], wb[:, dk, col0:col0 + 128], xnT[:, dk, c0:c1], start=(dk == 0), stop=(dk == 7)),
                      tuple(tiles) + tuple(rd_extra), (b_bank[bk],))

        def pieces(c0, c1):
            out = []
            c = c0
            while c < c1:
                out.append((c, min(c1, c + 512)))
                c += 512
            return out

        b_wl = B("wlru"); b_gs = B("gset"); b_xr = B("xr"); b_xc = B("xc"); b_xcb = B("xcb"); b_ra = B("ra"); b_iu = B("iu"); b_sh = B("sh")
        b_acc = B("acc4"); b_gg = B("gg")

        def lru_chunk(c, wsets, full, carry_in, carry_out, rev):
            for (a, b_) in pieces(0, SE):
                proj(wlrub, c * 128, a, b_, 0, (b_wl,))
                sc.op("act", lambda e, a=a, b_=b_: e.copy(out=xr_c[:, a:b_], in_=bank[0][:, 0:b_ - a]), (b_bank[0],), (b_xr,))
            if full:
                for (a, b_) in pieces(128, 128 + S):
                    proj(wlrub, 512 + c * 128, a, b_, 1, (b_wl,))
                    sc.op("act", lambda e, a=a, b_=b_: e.activation(out=gg[:, a - 128:b_ - 128], in_=bank[1][:, 0:b_ - a], func=AF.Gelu_apprx_tanh), (b_bank[1],), (b_gg,))
            gv = wsets[0][1]
            sc.op("dve", lambda e: e.tensor_scalar(out=xc_c[:, :], in0=xr_c[:, 126:126 + S], scalar1=gv[:, 16 + c * 5:17 + c * 5], scalar2=gv[:, 12 + c:13 + c], op0=ALU.mult, op1=ALU.add),
                  (b_xr, b_gs), (b_xc,))
            for j in range(1, 5):
                sc.op("dve", lambda e, j=j: e.scalar_tensor_tensor(out=xc_c[:, :], in0=xr_c[:, 126 + j:126 + j + S], scalar=gv[:, 16 + c * 5 + j:17 + c * 5 + j], in1=xc_c[:, :], op0=ALU.mult, op1=ALU.add),
                      (b_xr, b_gs, b_xc), (b_xc,))
            sc.op("act", lambda e: e.copy(out=xcb_c[:, :], in_=xc_c[:, :]), (b_xc,), (b_xcb,))
            for di, (gs_ap, gv_ap, cl_ap, rv) in enumerate(wsets):
                for (a, b_) in pieces(0, S):
                    sc.op("pe", lambda e, a=a, b_=b_, gs_ap=gs_ap: e.matmul(bank[2][:, 0:b_ - a], gs_ap[:, c * 128:(c + 1) * 128], xcb_c[:, a:b_], start=True, stop=True), (b_xcb, b_gs), (b_bank[2],))
                    sc.op("pe", lambda e, a=a, b_=b_, gs_ap=gs_ap: e.matmul(bank[3][:, 0:b_ - a], gs_ap[:, 512 + c * 128:512 + (c + 1) * 128], xcb_c[:, a:b_], start=True, stop=True), (b_xcb, b_gs), (b_bank[3],))
                    sc.op("act", lambda e, a=a, b_=b_, gv_ap=gv_ap: e.activation(out=ra[:, a:b_], in_=bank[2][:, 0:b_ - a], func=AF.Sigmoid, bias=gv_ap[:, c:c + 1]), (b_bank[2], b_gs), (b_ra,))
                    sc.op("act", lambda e, a=a, b_=b_, gv_ap=gv_ap: e.activation(out=iu[:, a:b_], in_=bank[3][:, 0:b_ - a], func=AF.Sigmoid, bias=gv_ap[:, 4 + c:5 + c]), (b_bank[3], b_gs), (b_iu,))
                sc.op("act", lambda e, cl_ap=cl_ap: e.activation(out=ra[:, :], in_=ra[:, :], func=AF.Exp, scale=cl_ap[:, c:c + 1]), (b_ra, b_gs), (b_ra,))
                sc.op("act", lambda e: e.activation(out=sh[:, :], in_=ra[:, :], func=AF.Square), (b_ra,), (b_sh,))
                sc.op("act", lambda e: e.activation(out=sh[:, :], in_=sh[:, :], func=AF.Sqrt, scale=-1.0, bias=1.0), (b_sh,), (b_sh,))
                sc.op("dve", lambda e: e.tensor_tensor(out=iu[:, :], in0=iu[:, :], in1=xc_c[:, :], op=ALU.mult), (b_iu, b_xc), (b_iu,))
                sc.op("dve", lambda e: e.tensor_tensor(out=iu[:, :], in0=iu[:, :], in1=sh[:, :], op=ALU.mult), (b_iu, b_sh), (b_iu,))
                ci = carry_in[di]
                if rv:
                    sc.op("dve", lambda e, ci=ci: e.tensor_tensor_scan(out=apx(sh, [[-1, S]], S - 1), data0=apx(ra, [[-1, S]], S - 1), data1=apx(iu, [[-1, S]], S - 1), initial=carries[:, ci, c:c + 1], op0=ALU.mult, op1=ALU.add),
                          (b_ra, b_iu, b_carr), (b_sh,))
                    last = sh[:, 0:1]
                else:
                    sc.op("dve", lambda e, ci=ci: e.tensor_tensor_scan(out=sh[:, :], data0=ra[:, :], data1=iu[:, :], initial=carries[:, ci, c:c + 1], op0=ALU.mult, op1=ALU.add),
                          (b_ra, b_iu, b_carr), (b_sh,))
                    last = sh[:, S - 1:S]
                co = carry_out[di]
                if co is not None:
                    sc.op("dve", lambda e, co=co, last=last: e.tensor_copy(out=carries[:, co, c:c + 1], in_=last), (b_sh, b_carr), (b_carr,))
                if full:
                    if di == 0:
                        sc.op("dve", lambda e: e.tensor_tensor(out=acc4[:, :], in0=gg[:, :], in1=sh[:, :], op=ALU.mult), (b_gg, b_sh), (b_acc,))
                    else:
                        sc.op("dve", lambda e: e.tensor_tensor(out=gg[:, :], in0=gg[:, :], in1=sh[:, :], op=ALU.mult), (b_gg, b_sh), (b_gg,))
                        sc.op("dve", lambda e: e.tensor_tensor(out=acc4[:, :], in0=acc4[:, :], in1=gg[:, :], op=ALU.add), (b_gg, b_acc), (b_acc,))
            if full:
                sc.op("act", lambda e: e.copy(out=lruT[:, c, :], in_=acc4[:, :]), (b_acc,), (b_lruT[c],))
                sc.op("act", lambda e: e.activation(out=acc4[:, :], in_=acc4[:, :], func=AF.Square), (b_acc,), (b_acc,))
                for j in range(T):
                    sc.op("pe", lambda e, j=j: e.matmul(bank[4][:, j:j + 1], acc4[:, j * 128:(j + 1) * 128], ones_f[:, 0:1], start=True, stop=True), (b_acc, b_c2), (b_bank[4],))
                if c == 0:
                    sc.op("dve", lambda e: e.tensor_copy(out=ssl[:, :], in_=bank[4][:, 0:T]), (b_bank[4],), (b_ssl,))
                else:
                    sc.op("dve", lambda e: e.tensor_tensor(out=ssl[:, :], in0=ssl[:, :], in1=bank[4][:, 0:T], op=ALU.add), (b_bank[4], b_ssl), (b_ssl,))

        def load_wlru():
            load_w(wlrub, w_lru, 8, NLRU, lambda dk: gcol[:, 0, dk:dk + 1], b_wl)

        def light_pass(slot, carry_in_idx, carry_out_idx):
            sc.barrier()
            load_wlru()
            dma("pool", "gsl", gsetl[:, :], gsets[2 + slot, :, 0:1024], (), (b_gs,))
            dma("sp", "gvl", gvecl[:, :], gsets[2 + slot, :, 1024:1064], (), (b_gs,))
            clam_compute(claml[:, :], gvecl[:, 8:12], (b_gs,), (b_gs,))
            phase_A1(x_light[slot], xinL, xnbL)
            for c in range(4):
                lru_chunk(c, [(gsetl, gvecl, claml, False)], False, [carry_in_idx], [carry_out_idx], False)

        for slot in range(6):
            if slot == 0:
                cin = 9
            else:
                sc.op("dve", lambda e, slot=slot: e.tensor_scalar(out=carries[:, 11, :], in0=carries[:, slot - 1, :], scalar1=cfg[:, slot:slot + 1], scalar2=None, op0=ALU.mult), (b_carr, b_consts), (b_carr,))
                cin = 11
            light_pass(slot, cin, slot)
        for (dst, off) in ((7, 8), (8, 16)):
            sc.op("dve", lambda e, dst=dst, off=off: e.tensor_scalar(out=carries[:, dst, :], in0=carries[:, 0, :], scalar1=cfg[:, off:off + 1], scalar2=None, op0=ALU.mult), (b_carr, b_consts), (b_carr,))
            for s_ in range(1, 6):
                sc.op("dve", lambda e, dst=dst, off=off, s_=s_: e.scalar_tensor_tensor(out=carries[:, dst, :], in0=carries[:, s_, :], scalar=cfg[:, off + s_:off + s_ + 1], in1=carries[:, dst, :], op0=ALU.mult, op1=ALU.add), (b_carr, b_consts), (b_carr,))
        light_pass(6, 8, 6)

        b_wq = B("wqkv"); b_rope = B("rope"); b_qr = sc.bufs_n("qr", T); b_kr = sc.bufs_n("kr", NT); b_v = sc.bufs_n("v", NT)
        b_E = sc.bufs_n("E", 3); b_rt = sc.bufs_n("ropet", 2); b_af = B("attnf"); b_ab = B("attnb"); b_bm = B("bmask")

        def full_pass(slot, cf_in, cb_in, cf_out):
            sc.barrier()
            load_w(wqkvb, w_qkv, 8, NQKV, lambda dk: gcol[:, 0, dk:dk + 1], b_wq)
            dma("sp", "rope", cosT[:, :], rope[slot, 0, :, :], (), (b_rope,))
            dma("sp", "rope", sinT[:, :], rope[slot, 1, :, :], (), (b_rope,))
            dma("sp", "bm", bmask[:, :, :], masks[2 + 2 * slot:4 + 2 * slot].rearrange("m p q -> p m q"), (), (b_bm,))
            phase_A1(x_full[slot], xin, xnb)

            def rope_piece(colq, colsw, a, b_, dst, wr):
                proj(wqkvb, colq, a, b_, 0, (b_wq,))
                proj(wqkvb, colsw, a, b_, 1, (b_wq,))
                n = b_ - a
                sc.op("dve", lambda e: e.tensor_tensor(out=ropet[0][:, 0:n], in0=bank[0][:, 0:n], in1=cosT[:, a:b_], op=ALU.mult), (b_bank[0], b_rope), (b_rt[0],))
                sc.op("dve", lambda e: e.tensor_tensor(out=ropet[1][:, 0:n], in0=bank[1][:, 0:n], in1=sinT[:, a:b_], op=ALU.mult), (b_bank[1], b_rope), (b_rt[1],))
                sc.op("dve", lambda e: e.tensor_tensor(out=dst, in0=ropet[0][:, 0:n], in1=ropet[1][:, 0:n], op=ALU.add), (b_rt[0], b_rt[1]), wr)

            for m in range(4):
                for (a, b_) in pieces(128, 128 + S):
                    rope_piece(m * 128, 512 + m * 128, a, b_, qrT[:, m, a - 128:b_ - 128], tuple(b_qr[j] for j in range((a - 128) // 128, (b_ - 128 + 127) // 128)))
            for (a, b_) in pieces(0, SE):
                rope_piece(1024, 1152, a, b_, krT[:, a:b_], tuple(b_kr[j] for j in range(a // 128, (b_ + 127) // 128)))
            for j in range(NT):
                for dk in range(8):
                    sc.op("pe", lambda e, dk=dk, j=j: e.matmul(bank[2][:, 0:128], xnT[:, dk, j * 128:(j + 1) * 128], wqkvb[:, dk, 1280:1408], start=(dk == 0), stop=(dk == 7)), (b_xnT[j], b_wq), (b_bank[2],))
                sc.op("act", lambda e, j=j: e.copy(out=apx(vaug, [[65, 2], [1, 64]], j * 130), in_=bank[2][:, 0:128].rearrange("p (g d) -> p g d", d=64)), (b_bank[2],), (b_v[j],))
                sc.op("dve", lambda e, j=j: e.memset(apx(vaug, [[65, 2], [1, 1]], j * 130 + 64), 1.0), (), (b_v[j],))
            for n in range(T):
                j = n + 1
                for g in range(2):
                    cbs = (j - 1, j, j + 1)
                    for ci, cb in enumerate(cbs):
                        sc.op("pe", lambda e, cb=cb, g=g, n=n, ci=ci: e.matmul(bank[ci][:, :], krT[64 * g:64 * g + 64, cb * 128:(cb + 1) * 128],
                                                                                apx(qrT[64 * g:64 * g + 64, 0, :], [[S, 4], [1, 128]], n * 128), start=True, stop=True),
                              (b_kr[cb], b_qr[n]), (b_bank[ci],))
                        sc.op("act", lambda e, ci=ci: e.activation(out=Et[ci][:, :], in_=bank[ci][:, :], func=AF.Exp, scale=0.125), (b_bank[ci],), (b_E[ci],))
                    for ci, mi in ((0, 0), (2, 1)):
                        if (ci == 0 and n == 0) or (ci == 2 and n == T - 1):
                            msrc = bmask[:, mi, :]; rdm = b_bm
                        else:
                            msrc = mstd[:, mi, :]; rdm = b_consts
                        sc.op("dve", lambda e, ci=ci, msrc=msrc: e.tensor_tensor(out=Et[ci][:, :].rearrange("p (h q) -> p h q", q=128), in0=Et[ci][:, :].rearrange("p (h q) -> p h q", q=128),
                                                                               in1=apx(msrc, [[0, 4], [1, 128]]), op=ALU.mult), (b_E[ci], rdm), (b_E[ci],))
                    ob = 3 + g
                    for hh in range(4):
                        for ci, cb in enumerate(cbs):
                            sc.op("pe", lambda e, hh=hh, ci=ci, cb=cb, g=g, ob=ob: e.matmul(bank[ob][:, hh * 65:(hh + 1) * 65], Et[ci][:, hh * 128:(hh + 1) * 128], vaug[:, cb, g * 65:(g + 1) * 65], start=(ci == 0), stop=(ci == 2)),
                                  (b_E[ci], b_v[cb]), (b_bank[ob],))
                    den = small[:, 8 + 4 * g:12 + 4 * g]
                    sc.op("dve", lambda e, ob=ob, g=g, den=den: e.tensor_tensor(out=den, in0=apx(bank[ob][:, 0:1], [[65, 4]], 64), in1=esink[:, 4 * g:4 * g + 4], op=ALU.add), (b_bank[ob], b_c2), (b_small,))
                    sc.op("dve", lambda e, den=den: e.reciprocal(out=den, in_=den), (b_small,), (b_small,))
                    sc.op("dve", lambda e, ob=ob, g=g, den=den: e.tensor_tensor(out=attnf[:, g * 256:(g + 1) * 256].rearrange("p (h d) -> p h d", d=64), in0=apx(bank[ob][:, 0:1], [[65, 4], [1, 64]]),
                                                                                in1=apx(den, [[1, 4], [0, 64]]), op=ALU.mult), (b_bank[ob], b_small), (b_af,))
                ss = small[:, 16:17]; rs = small[:, 17:18]
                rms_stats(b_ab, attnf[:, :], 512, attnb[:, :], ss, rs, (b_af,), b_small)
                sc.op("dve", lambda e, rs=rs: e.tensor_scalar(out=attnb[:, :], in0=attnf[:, :], scalar1=rs, scalar2=None, op0=ALU.mult), (b_af, b_ab, b_small), (b_ab,))
                transposes(attnb, 4, (b_ab,), attnT[:, :, n * 128:(n + 1) * 128], (b_attnT[n],))
            sc.barrier()
            load_wlru()
            for c in range(4):
                lru_chunk(c, [(gset01[:, 0, :], gvec01[:, 0, :], clam01[:, 0, :], False), (gset01[:, 1, :], gvec01[:, 1, :], clam01[:, 1, :], True)], True,
                          [cf_in, cb_in], [cf_out, None], True)
            sc.op("act", lambda e: e.activation(out=rstdl[:, :], in_=ssl[:, :], func=AF.Sqrt, scale=1.0 / 512, bias=EPS), (b_ssl,), (b_rstdl,))
            sc.op("dve", lambda e: e.reciprocal(out=rstdl[:, :], in_=rstdl[:, :]), (b_rstdl,), (b_rstdl,))
            sc.barrier()
            phase_B(slot)

        b_wpq = B("wpq"); b_keys = B("keys"); b_wg = B("wg"); b_wp = B("wp")
        b_xB = B("xB"); b_h1s = sc.bufs_n("h1", 2); b_xn2 = B("xn2"); b_xn2b = B("xn2b"); b_xn2T = B("xn2T"); b_qpT = B("qpT"); b_subs = B("subs"); b_subt = B("subt")
        b_uv = sc.bufs_n("uvbuf", 6); b_dg = sc.bufs_n("diag", 4); b_dk = sc.bufs_n("dk", 8); b_wk = sc.bufs_n("wk", 8); b_xgs = sc.bufs_n("xn2gb", 2); b_junk = B("junk"); b_junkL = B("junkL")
        b_pBs = sc.bufs_n("pB", 2); b_pBb = B("pBb"); b_pT = B("pT"); b_gate = B("gate")
        b_tk = B("topk"); b_bsm = B("bsmall"); b_eids = sc.bufs_n("eid", 2); b_gsms = sc.bufs_n("gsm", 2)

        def pre_gen(slot, n):
            par = n % 2
            j = n + 1
            h1 = h1s[par]; b_h1 = b_h1s[par]; pB = pBs[par]; b_pB = b_pBs[par]
            xn2gb = xn2gbs[par]; b_xg = b_xgs[par]; eidi = eidis[par]; b_eid = b_eids[par]; gsm = gsms[par]; b_gs_ = b_gsms[par]
            dma("sp", "xB", xB[:, :], x_full[slot, j * 128:(j + 1) * 128, :], (), (b_xB,))
            dma("sp", f"pB{par}", pB[:, :], p_in[slot, n * 128:(n + 1) * 128, :], (), (b_pB,))
            yield
            for half in range(2):
                for m in range(4):
                    sc.op("pe", lambda e, m=m, half=half: e.matmul(bank[half][:, :], attnT[:, m, n * 128:(n + 1) * 128], woutb[:, m, half * 512:(half + 1) * 512], start=(m == 0), stop=(m == 3)),
                          (b_attnT[n], b_wout), (b_bank[half],))
                for c in range(4):
                    sc.op("pe", lambda e, c=c, half=half: e.matmul(bank[2 + half][:, :], lruT[:, c, n * 128:(n + 1) * 128], woutb[:, 4 + c, half * 512:(half + 1) * 512], start=(c == 0), stop=(c == 3)),
                          (b_lruT[c], b_wout), (b_bank[2 + half],))
                hs = slice(half * 512, (half + 1) * 512)
                sc.op("dve", lambda e, half=half, hs=hs: e.tensor_tensor(out=h1[:, hs], in0=bank[half][:, :], in1=xB[:, hs], op=ALU.add), (b_bank[half], b_xB), (b_h1,))
                sc.op("dve", lambda e, half=half, hs=hs: e.scalar_tensor_tensor(out=h1[:, hs], in0=bank[2 + half][:, :], scalar=rstdl[:, n:n + 1], in1=h1[:, hs], op0=ALU.mult, op1=ALU.add),
                      (b_bank[2 + half], b_rstdl, b_h1), (b_h1,))
                yield
            ss = small[:, 20:21]; rs = small[:, 21:22]
            rms_stats(b_junk, h1[:, :], D, junk[:, :], ss, rs, (b_h1,), b_bsm)
            yield
            sc.op("dve", lambda e: e.tensor_scalar(out=xn2b[:, :], in0=h1[:, :], scalar1=rs, scalar2=None, op0=ALU.mult), (b_h1, b_bsm), (b_xn2b,))
            yield
            sc.op("dve", lambda e: e.scalar_tensor_tensor(out=xn2gb[:, :], in0=h1[:, :], scalar=rs, in1=grow[:, 0, :], op0=ALU.mult, op1=ALU.mult), (b_h1, b_bsm, b_consts), (b_xg,))
            yield
            transposes(xn2b, 8, (b_xn2b,), xn2T[:, :, :], (b_xn2T,))
            yield
            for grp in range(4):
                for cc in range(4):
                    ch = grp * 4 + cc
                    for dk in range(8):
                        sc.op("pe", lambda e, dk=dk, ch=ch, cc=cc, grp=grp: e.matmul(bank[grp][:, cc * 128:(cc + 1) * 128], wpqb[:, dk, ch * 128:(ch + 1) * 128], xn2T[:, dk, :], start=(dk == 0), stop=(dk == 7)),
                              (b_xn2T, b_wpq), (b_bank[grp],))
                sc.op("act", lambda e, grp=grp: e.copy(out=qpT[:, grp * 4:(grp + 1) * 4, :], in_=bank[grp][:, :].rearrange("p (a b) -> p a b", b=128)), (b_bank[grp],), (b_qpT,))
                yield
            for grp in range(4):
                for cc in range(4):
                    ch = grp * 4 + cc
                    sc.op("pe", lambda e, ch=ch, cc=cc, grp=grp: e.matmul(bank[grp][:, cc * 128:(cc + 1) * 128], qpT[:, ch, :], keysb[:, ch, :], start=True, stop=True), (b_qpT, b_keys), (b_bank[grp],))
                sc.op("act", lambda e, grp=grp: e.copy(out=subs[:, grp * 4:(grp + 1) * 4, :], in_=bank[grp][:, :].rearrange("p (a b) -> p a b", b=128)), (b_bank[grp],), (b_subs,))
                yield
            for ch in range(16):
                sc.op("dve", lambda e, ch=ch: e.max(out=top0[:, ch, 0:8], in_=subs[:, ch, :]), (b_subs,), (b_tk,))
                sc.op("dve", lambda e, ch=ch: e.max_index(out=idxu[:, ch, 0:8], in_max=top0[:, ch, 0:8], in_values=subs[:, ch, :]), (b_subs, b_tk), (b_tk,))
                sc.op("dve", lambda e, ch=ch: e.match_replace(out=subt[:, :], in_to_replace=top0[:, ch, 0:8], in_values=subs[:, ch, :], imm_value=-1e30), (b_subs, b_tk), (b_subt,))
                sc.op("dve", lambda e, ch=ch: e.max(out=top0[:, ch, 8:16], in_=subt[:, :]), (b_subt,), (b_tk,))
                sc.op("dve", lambda e, ch=ch: e.max_index(out=idxu[:, ch, 8:16], in_max=top0[:, ch, 8:16], in_values=subt[:, :]), (b_subt, b_tk), (b_tk,))
                yield
            sc.op("dve", lambda e: e.tensor_copy(out=idxf[:, :, :], in_=idxu[:, :, :]), (b_tk,), (b_tk,))
            sc.op("dve", lambda e: e.tensor_tensor(out=cand[:, :, :].rearrange("p h (a b) -> p h a b", b=16), in0=apx(top0[:, 0, :], [[32, 8], [1, 16], [0, 16]]), in1=apx(top0[:, 0, :], [[32, 8], [0, 16], [1, 16]], 16), op=ALU.add),
                  (b_tk,), (b_tk, b_subs))
            yield
            for h in range(8):
                sc.op("dve", lambda e, h=h: e.max(out=cbest[:, h, 0:8], in_=cand[:, h, :]), (b_tk,), (b_tk,))
                sc.op("dve", lambda e, h=h: e.max_index(out=posu[:, h, 0:8], in_max=cbest[:, h, 0:8], in_values=cand[:, h, :]), (b_tk,), (b_tk,))
                sc.op("dve", lambda e, h=h: e.match_replace(out=candt[:, :], in_to_replace=cbest[:, h, 0:8], in_values=cand[:, h, :], imm_value=-1e30), (b_tk,), (b_subt,))
                sc.op("dve", lambda e, h=h: e.max(out=cbest[:, h, 8:16], in_=candt[:, :]), (b_subt,), (b_tk,))
                sc.op("dve", lambda e, h=h: e.max_index(out=posu[:, h, 8:16], in_max=cbest[:, h, 8:16], in_values=candt[:, :]), (b_subt, b_tk), (b_tk,))
                yield
            sc.op("dve", lambda e: e.tensor_single_scalar(out=posf[:, :, :].bitcast(U32), in_=posu[:, :, :], scalar=4, op=ALU.logical_shift_right), (b_tk,), (b_tk,))
            sc.op("dve", lambda e: e.tensor_copy(out=afl[:, :, :], in_=posf[:, :, :].bitcast(U32)), (b_tk,), (b_tk,))
            sc.op("dve", lambda e: e.tensor_single_scalar(out=posf[:, :, :].bitcast(U32), in_=posu[:, :, :], scalar=15, op=ALU.bitwise_and), (b_tk,), (b_tk,))
            sc.op("dve", lambda e: e.tensor_copy(out=bfl[:, :, :], in_=posf[:, :, :].bitcast(U32)), (b_tk,), (b_tk,))
            yield
            for (sel, which, dst) in ((afl, 0, isel), (bfl, 1, jsel)):
                sc.op("dve", lambda e, sel=sel: e.tensor_tensor(out=oh[:, :, :], in0=apx(sel[:, 0, :], [[1, 128], [0, 16]]), in1=apx(iota16[:, :], [[0, 128], [1, 16]]), op=ALU.is_equal), (b_tk, b_consts), (b_tk, b_subs))
                yield
                sc.op("dve", lambda e, which=which: e.tensor_tensor(out=oh[:, :, :].rearrange("p (h k) a -> p h k a", k=16), in0=oh[:, :, :].rearrange("p (h k) a -> p h k a", k=16),
                                                                      in1=apx(idxf[:, 0, :], [[32, 8], [0, 16], [1, 16]], 16 * which), op=ALU.mult), (b_tk,), (b_tk,))
                yield
                sc.op("dve", lambda e, dst=dst: e.tensor_reduce(out=dst[:, :], in_=oh[:, :, :], axis=AX.X, op=ALU.add), (b_tk,), (b_tk,))
                yield
            sc.op("dve", lambda e: e.scalar_tensor_tensor(out=eidf[:, :], in0=isel[:, :], scalar=128.0, in1=jsel[:, :], op0=ALU.mult, op1=ALU.add), (b_tk,), (b_tk,))
            sc.op("dve", lambda e: e.tensor_copy(out=eidi[:, :], in_=eidf[:, :]), (b_tk,), (b_eid,))
            yield
            sc.op("dve", lambda e: e.tensor_tensor(out=gsm[:, :, :], in0=cbest[:, :, :], in1=apx(cbest[:, 0, :], [[16, 8], [0, 16]]), op=ALU.subtract), (b_tk,), (b_gs_,))
            sc.op("act", lambda e: e.activation(out=gsm[:, :, :], in_=gsm[:, :, :], func=AF.Exp), (b_gs_,), (b_gs_,))
            sc.op("dve", lambda e: e.tensor_reduce(out=small[:, 24:32], in_=gsm[:, :, :], axis=AX.X, op=ALU.add), (b_gs_,), (b_bsm,))
            sc.op("dve", lambda e: e.reciprocal(out=small[:, 24:32], in_=small[:, 24:32]), (b_bsm,), (b_bsm,))
            sc.op("dve", lambda e: e.tensor_tensor(out=gsm[:, :, :], in0=gsm[:, :, :], in1=apx(small[:, 24:32], [[1, 8], [0, 16]]), op=ALU.mult), (b_gs_, b_bsm), (b_gs_,))
            yield

        def loop_iter(n, kk):
            par = n % 2
            xn2gb = xn2gbs[par]; b_xg = b_xgs[par]; eidi = eidis[par]; b_eid = b_eids[par]; gsm = gsms[par]; b_gs_ = b_gsms[par]
            gflat = gsm[:, :, :].rearrange("p h k -> p (h k)")
            s_ = kk % 6; s2 = kk % 4; s8 = kk % 8
            o = sc.op("pool", lambda e: e.indirect_dma_start(out=uvbuf[s_][:, :], out_offset=None, in_=uvb[:, :], in_offset=bass.IndirectOffsetOnAxis(ap=eidi[:, kk:kk + 1], axis=0)),
                      (b_eid,), (b_uv[s_],), dma=f"ug{s_}")
            if uvb_ops:
                o.deps.extend(uvb_ops)
                uvb_ops.clear()
            sc.op("dve", lambda e: e.scalar_tensor_tensor(out=junkL[:, :], in0=uvbuf[s_][:, 0:D], scalar=1.0, in1=xn2gb[:, :], op0=ALU.mult, op1=ALU.mult, accum_out=dots[:, kk:kk + 1]),
                  (b_uv[s_], b_xg), (b_junkL, b_dk[s8]))
            sc.op("act", lambda e: e.activation(out=wts[:, kk:kk + 1], in_=dots[:, kk:kk + 1], func=AF.Gelu_apprx_tanh), (b_dk[s8],), (b_wk[s8],))
            sc.op("dve", lambda e: e.tensor_scalar(out=diag[s2][:, :], in0=ident[:, :], scalar1=wts[:, kk:kk + 1], scalar2=gflat[:, kk:kk + 1], op0=ALU.mult, op1=ALU.mult),
                  (b_wk[s8], b_gs_, b_consts), (b_dg[s2],))
            for half in range(2):
                sc.op("pe", lambda e, half=half: e.matmul(bank[5 + half][:, :], diag[s2][:, :], uvbuf[s_][:, D + half * 512:D + (half + 1) * 512], start=(kk == 0), stop=(kk == 127)),
                      (b_dg[s2], b_uv[s_]), (b_bank[5 + half],))

        def post_gen(slot, n):
            par = n % 2
            h1 = h1s[par]; b_h1 = b_h1s[par]; pB = pBs[par]; b_pB = b_pBs[par]
            ss = small[:, 22:23]; rs = small[:, 23:24]
            rms_stats(b_junk, h1[:, :], D, junk[:, :], ss, rs, (b_h1,), b_bsm)
            yield
            sc.op("dve", lambda e: e.tensor_scalar(out=xn2b[:, :], in0=h1[:, :], scalar1=rs, scalar2=None, op0=ALU.mult), (b_h1, b_bsm), (b_xn2b,))
            yield
            transposes(xn2b, 8, (b_xn2b,), xn2T[:, :, :], (b_xn2T,))
            yield
            sc.op("dve", lambda e: e.tensor_copy(out=pBb[:, :], in_=pB[:, :]), (b_pB,), (b_pBb,))
            transposes(pBb, 2, (b_pBb,), pT[:, :, :], (b_pT,))
            yield
            for half in range(2):
                for dk in range(8):
                    sc.op("pe", lambda e, dk=dk, half=half: e.matmul(bank[half][:, :], xn2T[:, dk, :], wgb[:, dk, half * 512:(half + 1) * 512], start=(dk == 0), stop=(dk == 7)), (b_xn2T, b_wg), (b_bank[half],))
                for kc in range(2):
                    sc.op("pe", lambda e, kc=kc, half=half: e.matmul(bank[2 + half][:, :], pT[:, kc, :], wpb[:, kc, half * 512:(half + 1) * 512], start=(kc == 0), stop=(kc == 1)), (b_pT, b_wp), (b_bank[2 + half],))
                hs = slice(half * 512, (half + 1) * 512)
                sc.op("act", lambda e, half=half, hs=hs: e.activation(out=gate[:, hs], in_=bank[half][:, :], func=AF.Sigmoid), (b_bank[half],), (b_gate,))
                yield
                sc.op("dve", lambda e, half=half, hs=hs: e.tensor_tensor(out=gate[:, hs], in0=gate[:, hs], in1=bank[2 + half][:, :], op=ALU.mult), (b_gate, b_bank[2 + half]), (b_gate,))
                yield
            sc.op("dve", lambda e: e.tensor_tensor(out=h1[:, :], in0=h1[:, :], in1=gate[:, :], op=ALU.add), (b_h1, b_gate), (b_h1,))
            yield
            ss2 = small[:, 32:33]; rs2 = small[:, 33:34]
            rms_stats(b_junk, h1[:, :], D, junk[:, :], ss2, rs2, (b_h1,), b_bsm)
            yield
            sc.op("dve", lambda e: e.scalar_tensor_tensor(out=gate[:, :], in0=h1[:, :], scalar=rs2, in1=grow[:, 1, :], op0=ALU.mult, op1=ALU.mult), (b_h1, b_bsm, b_consts), (b_gate,))
            dma("sp", "yout", y_out[slot, n * 128:(n + 1) * 128, :], gate[:, :], (b_gate,), ())
            yield

        def phase_B(slot):
            load_w(wpqb, w_pq, 8, 2048, lambda dk: gcol[:, 1, dk:dk + 1], b_wpq)
            load_w(wgb, w_gate, 8, D, lambda dk: gcol[:, 2, dk:dk + 1], b_wg)
            load_w(wpb, w_proj, 2, D, None, b_wp)
            dma("pool", "keys", keysb[:, :, :], keysT[:, :, :], (), (b_keys,))
            for _ in pre_gen(slot, 0):
                pass
            prev_post = None
            for n in range(T):
                gens = []
                if prev_post is not None:
                    gens.append(prev_post)
                if n + 1 < T:
                    gens.append(pre_gen(slot, n + 1))

                def side():
                    for g in gens:
                        yield from g
                side_it = side()
                for kk in range(128):
                    loop_iter(n, kk)
                    next(side_it, None)
                par = n % 2
                for half in range(2):
                    hs = slice(half * 512, (half + 1) * 512)
                    sc.op("dve", lambda e, half=half, hs=hs, par=par: e.tensor_tensor(out=h1s[par][:, hs], in0=h1s[par][:, hs], in1=bank[5 + half][:, :], op=ALU.add), (b_h1s[par], b_bank[5 + half]), (b_h1s[par],))
                for _ in side_it:
                    pass
                prev_post = post_gen(slot, n)
            for _ in prev_post:
                pass

        for slot in range(4):
            full_pass(slot, 9, 9, None)
        full_pass(4, 7, 6, 10)
        full_pass(5, 10, 8, None)

        sc.emit(nc, stack)
    return nc, sc


def _cols(a):
    return np.ascontiguousarray(a.reshape(-1, 128).T)


def _prep_shared(inp, T):
    f = np.float32
    w_in = inp["w_in"][0]
    qcols = np.zeros(512, np.int64); qsw = np.zeros(512, np.int64)
    for m in range(4):
        for p in range(128):
            head = m + 4 * (p // 64); d = p % 64
            qcols[m * 128 + p] = head * 64 + d
            qsw[m * 128 + p] = head * 64 + (d + 32) % 64
    kcols = 512 + np.arange(128)
    ksw = 512 + (np.arange(128) // 64) * 64 + (np.arange(128) % 64 + 32) % 64
    vcols = 640 + np.arange(128)
    w_qkv = np.ascontiguousarray(w_in[:, np.concatenate([qcols, qsw, kcols, ksw, vcols])])
    w_lru = np.ascontiguousarray(w_in[:, 768:1792])
    g_cols = np.zeros((128, 4, 8), f)
    g_cols[:, 0] = _cols(inp["mix_norm_g"][0]); g_cols[:, 1] = _cols(inp["ffn_norm_g"][0]); g_cols[:, 2] = _cols(inp["ple_norm_g"][0])
    g_rows = np.stack([inp["ffn_norm_g"][0], inp["final_norm_g"]]).astype(f)
    og_cols = np.concatenate([_cols(inp["attn_out_norm_g"][0]), _cols(inp["lru_out_norm_g"][0])], axis=1).astype(f)
    sink = inp["attn_sink"].reshape(1, 8).astype(f)
    keysT = np.ascontiguousarray(inp["peer_keys"][0].reshape(16, 128, 128).transpose(2, 0, 1))
    ident = np.eye(128, dtype=f).astype(ml_dtypes.bfloat16)
    iota16 = np.tile(np.arange(16, dtype=f)[None, :], (128, 1))

    def gset(direction, flip):
        g = np.zeros((128, 8 * 128 + 40), f)
        for wi, wname in enumerate(("lru_wa", "lru_wx")):
            w = inp[wname][0, direction]
            for c in range(4):
                for bb in range(2):
                    blk = 2 * c + bb
                    g[bb * 64:(bb + 1) * 64, wi * 512 + c * 128 + bb * 64: wi * 512 + c * 128 + (bb + 1) * 64] = w[blk]
        g[:, 1024:1028] = _cols(inp["lru_ba"][0, direction]); g[:, 1028:1032] = _cols(inp["lru_bx"][0, direction])
        g[:, 1032:1036] = _cols(inp["lru_lambda"][0, direction]); g[:, 1036:1040] = _cols(inp["conv_b"][0])
        cw = inp["conv_w"][0]
        w5 = np.zeros((5, 512), f)
        if flip:
            w5[1:5] = cw[::-1]
        else:
            w5[0:4] = cw
        for c in range(4):
            g[:, 1040 + c * 5:1040 + (c + 1) * 5] = w5[:, c * 128:(c + 1) * 128].T
        return g
    return dict(w_qkv=w_qkv, w_lru=w_lru, g_cols=g_cols, g_rows=g_rows, og_cols=og_cols, sink=sink, keysT=keysT, ident=ident,
                iota16=iota16, w_out=np.ascontiguousarray(inp["w_out"][0]), w_pq=np.ascontiguousarray(inp["peer_wq"][0]),
                peer_u=inp["peer_u"][0], peer_v=inp["peer_v"][0], w_gate=np.ascontiguousarray(inp["ple_w_gate"][0]),
                w_proj=np.ascontiguousarray(inp["ple_w_proj"][0])), gset


def _seg(xseq, start, S):
    n = xseq.shape[0]
    out = np.zeros((S + 256, xseq.shape[1]), np.float32)
    lo = max(0, start - 128); hi = min(n, start + S + 128)
    out[lo - (start - 128):hi - (start - 128)] = xseq[lo:hi]
    return out


def _rope_tab(start, SE):
    pos = (np.arange(SE, dtype=np.float32) + np.float32(start - 128)).astype(np.float32)
    half = 32
    inv = (np.float32(10000.0) ** (-np.arange(half, dtype=np.float32) / np.float32(half))).astype(np.float32)
    ang = (pos[None, :] * inv[:, None]).astype(np.float32)
    c = np.cos(ang).astype(np.float32); s_ = np.sin(ang).astype(np.float32)
    cosT = np.zeros((128, SE), np.float32); sinT = np.zeros((128, SE), np.float32)
    for p in range(128):
        d = p % 64
        cosT[p] = c[d % 32]
        sinT[p] = -s_[d % 32] if d < 32 else s_[d % 32]
    return np.stack([cosT, sinT])


def kernel(**inp):
    inp = {k: np.asarray(v) for k, v in inp.items()}
    xp = inp["x_prompt"]; xs = inp["x_sample"]; pp = inp["p_prompt"][0]; ps = inp["p_sample"][0]
    S = xp.shape[1]; T = S // 128; SE = S + 256
    assert xp.shape[0] == 32 and xs.shape[0] == 2 and xs.shape[1] == 8 * S
    shared, gset = _prep_shared(inp, T)
    cq = np.arange(128)[:, None]; qq = np.arange(128)[None, :]
    m_prev = (cq >= qq).astype(np.float32); m_next = (cq <= qq).astype(np.float32); m_zero = np.zeros((128, 128), np.float32)
    rope_p = _rope_tab(0, SE)
    in_maps = []
    for c in range(NCORES):
        sq, q = c // 4, c % 4
        xseq = xs[sq]; xrev = xseq[::-1]
        t0 = q * 2 * S
        x_full = np.stack([_seg(xp[4 * c + k], 0, S) for k in range(4)] + [_seg(xseq, t0, S), _seg(xseq, t0 + S, S)])
        p_in = np.stack([pp[4 * c + k] for k in range(4)] + [ps[sq, t0:t0 + S], ps[sq, t0 + S:t0 + 2 * S]])
        nF = 2 * q; nB = 6 - 2 * q
        lights = [_seg(xseq, s_ * S, S) for s_ in range(nF)] + [_seg(xrev, r * S, S) for r in range(nB)] + [_seg(xrev, nB * S, S)]
        x_light = np.stack(lights)
        gs = [gset(0, False), gset(1, False)] + [gset(0, False)] * nF + [gset(1, True)] * nB + [gset(1, True)]
        cfg = np.zeros((128, 32), np.float32)
        for s_ in range(6):
            cfg[:, s_] = 0.0 if (s_ == 0 or s_ == nF) else 1.0
        if nF > 0:
            cfg[:, 8 + nF - 1] = 1.0
        if nB > 0:
            cfg[:, 16 + 5] = 1.0
        rope = np.stack([rope_p] * 4 + [_rope_tab(t0, SE), _rope_tab(t0 + S, SE)])
        mk = [m_prev, m_next]
        for k in range(4):
            mk += [m_zero, m_zero]
        mk += [m_prev if q > 0 else m_zero, m_prev]
        mk += [m_prev, m_next if q < 3 else m_zero]
        mk[2 + 2 * 4 + 1] = m_next
        masks = np.stack(mk).astype(ml_dtypes.bfloat16)
        m = dict(shared)
        m.update(x_full=x_full, x_light=x_light, p_in=np.ascontiguousarray(p_in), gsets=np.stack(gs), cfg=cfg, rope=rope, masks=masks)
        in_maps.append(m)
    nc, sc = build_program(T)
    res = run_bass_kernel_spmd(nc, in_maps, core_ids=list(range(NCORES)))
    y_prompt = np.zeros(xp.shape, np.float32); y_sample = np.zeros(xs.shape, np.float32)
    for c in range(NCORES):
        y = res.results[c]["y_out"]
        sq, q = c // 4, c % 4
        t0 = q * 2 * S
        for k in range(4):
            y_prompt[4 * c + k] = y[k]
        y_sample[sq, t0:t0 + S] = y[4]
        y_sample[sq, t0 + S:t0 + 2 * S] = y[5]
    return (y_prompt, y_sample)
```

```python
from contextlib import ExitStack
import numpy as np
import ml_dtypes
import concourse.bass as bass
import concourse.mybir as mybir
from concourse.bass_utils import run_bass_kernel_spmd

F32 = mybir.dt.float32
BF16 = mybir.dt.bfloat16
I32 = mybir.dt.int32
U32 = mybir.dt.uint32
AF = mybir.ActivationFunctionType
ALU = mybir.AluOpType
AX = mybir.AxisListType

D = 1024
NCORES = 8
EPS = 1e-6
NQKV = 1408
NLRU = 1024


class Buf:
    __slots__ = ("name", "w", "r")

    def __init__(self, name):
        self.name = name
        self.w = None
        self.r = []


class Op:
    __slots__ = ("eng", "fn", "deps", "dma", "sig", "needed", "idx")

    def __init__(self, eng, fn, dma):
        self.eng = eng
        self.fn = fn
        self.deps = []
        self.dma = dma
        self.sig = None
        self.needed = False


class Sched:
    ENGS = ("pe", "act", "dve", "pool", "sp")

    def __init__(self):
        self.ops = []
        self.bufs = []
        self.bar = []
        self.last = {}
        self.dma_last = {}

    def buf(self, name):
        b = Buf(name)
        self.bufs.append(b)
        return b

    def bufs_n(self, name, n):
        return [self.buf(f"{name}{i}") for i in range(n)]

    def op(self, eng, fn, reads=(), writes=(), dma=None):
        o = Op(eng, fn, dma)
        deps = list(self.bar)
        for b in reads:
            if b.w is not None:
                deps.append(b.w)
        for b in writes:
            if b.w is not None:
                deps.append(b.w)
            deps.extend(b.r)
        o.deps = deps
        for b in reads:
            b.r.append(o)
        for b in writes:
            b.w = o
            b.r = []
        self.ops.append(o)
        self.last[eng] = o
        if dma is not None:
            self.dma_last[dma] = o
        return o

    def barrier(self):
        self.bar = list(self.last.values()) + list(self.dma_last.values())
        for b in self.bufs:
            b.w = None
            b.r = []

    def emit(self, nc, stack, final_eng="sp"):
        fin = Op(final_eng, None, None)
        fin.deps = list(self.last.values()) + list(self.dma_last.values())
        self.ops.append(fin)
        for o in self.ops:
            for d in o.deps:
                d.needed = True
        MAXC = 30000
        eng_sems = {e: [] for e in self.ENGS}
        eng_cnt = {e: MAXC for e in self.ENGS}
        dma_sems = {}
        dma_cnt = {}
        nsem = [0]

        def newsem(nm):
            nsem[0] += 1
            return stack.enter_context(nc.semaphore(f"{nm}_{nsem[0]}"))

        for o in self.ops:
            if o.dma is not None:
                if o.dma not in dma_sems or dma_cnt[o.dma] >= 32000:
                    dma_sems[o.dma] = newsem("d")
                    dma_cnt[o.dma] = 0
                dma_cnt[o.dma] += 16
                o.sig = (dma_sems[o.dma], dma_cnt[o.dma])
            elif o.needed:
                if eng_cnt[o.eng] >= MAXC:
                    eng_sems[o.eng].append(newsem(o.eng))
                    eng_cnt[o.eng] = 0
                eng_cnt[o.eng] += 1
                o.sig = (eng_sems[o.eng][-1], eng_cnt[o.eng])
        per_eng = {e: [] for e in self.ENGS}
        for o in self.ops:
            per_eng[o.eng].append(o)
        self.n_ops = len(self.ops)

        def run(eng_name, eng):
            waited = {}
            for o in per_eng[eng_name]:
                for d in o.deps:
                    if d.dma is None and d.eng == "pe" and eng_name == "pe":
                        continue
                    sem, val = d.sig
                    k = id(sem)
                    if waited.get(k, 0) >= val:
                        continue
                    waited[k] = val
                    eng.wait_ge(sem, val)
                if o.fn is None:
                    continue
                inst = o.fn(eng)
                if o.dma is not None:
                    inst.then_inc(o.sig[0], 16)
                elif o.sig is not None:
                    inst.then_inc(o.sig[0], 1)

        with nc.Block() as block:
            @block.sync
            def _(e):
                run("sp", e)

            @block.scalar
            def _(e):
                run("act", e)

            @block.vector
            def _(e):
                run("dve", e)

            @block.gpsimd
            def _(e):
                run("pool", e)

            @block.tensor
            def _(e):
                run("pe", e)


def apx(base, dims, off=0):
    return bass.AP(tensor=base.tensor, offset=base.offset + off, ap=[list(base.ap[0])] + [list(d) for d in dims])


def build_program(T):
    NT = T + 2
    S = T * 128
    SE = NT * 128
    NFULL = 6
    NLIGHT = 7
    NPIECE = (S + 511) // 512
    PW = min(512, S)
    nc = bass.Bass("TRN2", target_bir_lowering=False)
    dt_in = lambda n, s, d=F32: nc.dram_tensor(n, s, d, kind="ExternalInput").ap()
    x_full = dt_in("x_full", [NFULL, SE, D])
    x_light = dt_in("x_light", [NLIGHT, SE, D])
    p_in = dt_in("p_in", [NFULL, S, 256])
    w_qkv = dt_in("w_qkv", [D, NQKV])
    w_lru = dt_in("w_lru", [D, NLRU])
    g_cols = dt_in("g_cols", [128, 4, 8])
    g_rows = dt_in("g_rows", [2, D])
    og_cols = dt_in("og_cols", [128, 8])
    sink = dt_in("sink", [1, 8])
    gsets = dt_in("gsets", [2 + NLIGHT, 128, 8 * 128 + 40])
    w_out = dt_in("w_out", [D, D])
    w_pq = dt_in("w_pq", [D, 2048])
    keysT = dt_in("keysT", [128, 16, 128])
    peer_u = dt_in("peer_u", [16384, D])
    peer_v = dt_in("peer_v", [16384, D])
    w_gate = dt_in("w_gate", [D, D])
    w_proj = dt_in("w_proj", [256, D])
    rope = dt_in("rope", [NFULL, 2, 128, SE])
    masks = dt_in("masks", [2 + 2 * NFULL, 128, 128], BF16)
    ident_in = dt_in("ident", [128, 128], BF16)
    cfg_in = dt_in("cfg", [128, 32])
    iota_in = dt_in("iota16", [128, 16])
    y_out = nc.dram_tensor("y_out", [NFULL, S, D], F32, kind="ExternalOutput").ap()
    uvb = nc.dram_tensor("uvb", [16384, 2 * D], BF16, kind="Internal").ap()

    sc = Sched()
    stack = ExitStack()
    with stack:
        sb = lambda n, s, d=F32: stack.enter_context(nc.sbuf_tensor(n, s, d))
        ident = sb("ident_sb", [128, 128], BF16)
        woutb = sb("woutb", [128, 8, D], BF16)
        lruT = sb("lruT", [128, 4, S], BF16)
        attnT = sb("attnT", [128, 4, S], BF16)
        grow = sb("grow", [128, 2, D])
        gcol = sb("gcol", [128, 4, 8])
        ogcol = sb("ogcol", [128, 8])
        esink = sb("esink", [128, 8])
        mstd = sb("mstd", [128, 2, 128], BF16)
        cfg = sb("cfg_sb", [128, 32])
        iota16 = sb("iota_sb", [128, 16])
        gset01 = sb("gset01", [128, 2, 8 * 128], BF16)
        gvec01 = sb("gvec01", [128, 2, 40])
        clam01 = sb("clam01", [128, 2, 4])
        rstdl = sb("rstdl", [128, T])
        ssl = sb("ssl", [128, T])
        ones_f = sb("ones_f", [128, 1])
        carries = sb("carries", [128, 16, 4])
        small = sb("small", [128, 64])
        stage = sb("stage", [128, 2048])
        RB = 70400
        R = sb("R", [128, RB], BF16)
        cur = [0]

        def carve(shape, dt):
            n = int(np.prod(shape))
            units = n * (2 if dt in (F32, I32, U32) else 1)
            a = R[:, cur[0]:cur[0] + units]
            cur[0] += units + (units % 2)
            assert cur[0] <= RB, (cur[0], RB)
            if dt != BF16:
                a = a.bitcast(dt)
            if len(shape) == 2:
                return a.rearrange("p (a b) -> p a b", b=shape[1])
            if len(shape) == 3:
                return a.rearrange("p (a b c) -> p a b c", b=shape[1], c=shape[2])
            return a

        xnT = carve([8, SE], BF16)
        markX = cur[0]
        wqkvb = carve([8, NQKV], BF16)
        cosT = carve([SE], F32)
        sinT = carve([SE], F32)
        qrT = carve([4, S], BF16)
        krT = carve([SE], BF16)
        vaug = carve([NT, 130], BF16)
        Et = [carve([512], BF16) for _ in range(3)]
        ropet = [carve([512], F32) for _ in range(2)]
        attnf = carve([512], F32)
        attnb = carve([512], BF16)
        bmask = carve([2, 128], BF16)
        xin = [carve([D], F32) for _ in range(2)]
        xnb = [carve([D], BF16) for _ in range(2)]
        endA2 = cur[0]
        cur[0] = markX
        wlrub = carve([8, NLRU], BF16)
        xr_c = carve([SE], F32)
        xc_c = carve([S], F32)
        xcb_c = carve([S], BF16)
        ra = carve([S], F32)
        iu = carve([S], F32)
        sh = carve([S], F32)
        acc4 = carve([S], F32)
        gg = carve([S], F32)
        gsetl = carve([8 * 128], BF16)
        gvecl = carve([40], F32)
        claml = carve([4], F32)
        xinL = [carve([D], F32) for _ in range(2)]
        xnbL = [carve([D], BF16) for _ in range(2)]
        endLRU = cur[0]
        cur[0] = 0
        wpqb = carve([8, 2048], BF16)
        keysb = carve([16, 128], BF16)
        wgb = carve([8, D], BF16)
        wpb = carve([2, D], BF16)
        xB = carve([D], F32)
        h1s = [carve([D], F32) for _ in range(2)]
        xn2b = carve([D], BF16)
        xn2T = carve([8, 128], BF16)
        qpT = carve([16, 128], BF16)
        subs = carve([16, 128], F32)
        subt = carve([128], F32)
        stage_b = stage[:, :].bitcast(BF16)
        uvbuf = [carve([2 * D], BF16) for _ in range(6)] + [stage_b[:, 0:2 * D], stage_b[:, 2 * D:4 * D]]
        NSLOT = 8
        xn2gbs = [carve([D], BF16) for _ in range(2)]
        junkL = carve([D], BF16)
        diag = [carve([128], BF16) for _ in range(4)]
        junk = carve([D], BF16)
        pBs = [carve([256], F32) for _ in range(2)]
        pBb = carve([256], BF16)
        pT = carve([2, 128], BF16)
        gate = carve([D], F32)
        top0 = carve([16, 16], F32)
        idxu = carve([16, 16], U32)
        idxf = carve([16, 16], F32)
        cand = subs.rearrange("p a b -> p (a b)").rearrange("p (h c) -> p h c", c=256)
        candt = carve([256], F32)
        cbest = carve([8, 16], F32)
        posu = carve([8, 16], U32)
        posf = carve([8, 16], F32)
        afl = carve([8, 16], F32)
        bfl = carve([8, 16], F32)
        oh = subs.rearrange("p a b -> p (a b)").rearrange("p (k a) -> p k a", a=16)
        isel = carve([128], F32)
        jsel = carve([128], F32)
        eidf = carve([128], F32)
        eidis = [carve([128], I32) for _ in range(2)]
        gsms = [carve([8, 16], F32) for _ in range(2)]
        dots = carve([128], F32)
        wts = carve([128], F32)
        endB = cur[0]

        pst = stack.enter_context(nc.psum_tensor("pst", [128, 1024], BF16))
        bank = [stack.enter_context(nc.psum_tensor(f"bank{i}", [128, 512], F32)) for i in range(7)]

        B = sc.buf
        b_pst = B("pst")
        b_bank = sc.bufs_n("bank", 7)
        b_stage = B("stage")
        b_wout = B("wout"); b_lruT = [B(f"lruT{c}") for c in range(4)]; b_attnT = [B(f"attnT{j}") for j in range(T)]
        b_consts = B("consts")
        b_small = B("small")
        b_carr = B("carries")
        b_rstdl = B("rstdl"); b_ssl = B("ssl")

        def dma(eng, key, out, in_, reads=(), writes=()):
            return sc.op(eng, lambda e, out=out, in_=in_: e.dma_start(out=out, in_=in_), reads, writes, dma=key)

        dma("sp", "c0", ident[:], ident_in[:, :], (), (b_consts,))
        dma("sp", "c0", grow[:, 0, :], g_rows[0:1, :].partition_broadcast(128), (), (b_consts,))
        dma("sp", "c0", grow[:, 1, :], g_rows[1:2, :].partition_broadcast(128), (), (b_consts,))
        dma("sp", "c0", gcol[:], g_cols[:, :, :], (), (b_consts,))
        dma("sp", "c0", ogcol[:], og_cols[:, :], (), (b_consts,))
        dma("sp", "c0", esink[:], sink[0:1, :].partition_broadcast(128), (), (b_consts,))
        dma("sp", "c0", mstd[:], masks[0:2].rearrange("m p q -> p m q"), (), (b_consts,))
        dma("sp", "c0", cfg[:], cfg_in[:, :], (), (b_consts,))
        dma("sp", "c0", iota16[:], iota_in[:, :], (), (b_consts,))
        for d_ in range(2):
            dma("pool", "c1", gset01[:, d_, :], gsets[d_, :, 0:1024], (), (b_consts,))
            dma("sp", "c0", gvec01[:, d_, :], gsets[d_, :, 1024:1064], (), (b_consts,))
        b_c2 = B("consts2")
        sc.op("act", lambda e: e.activation(out=esink[:], in_=esink[:], func=AF.Exp), (b_consts,), (b_c2,))
        sc.op("dve", lambda e: e.memset(ones_f[:], 1.0), (), (b_c2,))
        sc.op("dve", lambda e: e.memset(carries[:], 0.0), (), (b_carr,))

        def clam_compute(dst, lamsrc, rd, wr):
            sc.op("act", lambda e: e.activation(out=dst, in_=lamsrc, func=AF.Exp, scale=-1.0), rd, wr)
            sc.op("act", lambda e: e.activation(out=dst, in_=dst, func=AF.Ln, bias=1.0), wr, wr)
            sc.op("act", lambda e: e.mul(out=dst, in_=dst, mul=-8.0), wr, wr)

        for d_ in range(2):
            clam_compute(clam01[:, d_, :], gvec01[:, d_, 8:12], (b_consts,), (b_c2,))

        def load_w(dst, src, K, N, gain, bw):
            for dk in range(K):
                for n0 in range(0, N, 2048):
                    n1 = min(N, n0 + 2048)
                    dma("sp", "stg", stage[:, 0:n1 - n0], src[dk * 128:(dk + 1) * 128, n0:n1], (), (b_stage,))
                    if gain is None:
                        sc.op("dve", lambda e, dk=dk, n0=n0, n1=n1: e.tensor_copy(out=dst[:, dk, n0:n1], in_=stage[:, 0:n1 - n0]), (b_stage, b_consts), (bw,))
                    else:
                        sc.op("dve", lambda e, dk=dk, n0=n0, n1=n1, gain=gain: e.tensor_scalar(out=dst[:, dk, n0:n1], in0=stage[:, 0:n1 - n0], scalar1=gain(dk), scalar2=None, op0=ALU.mult), (b_stage, b_consts), (bw,))

        load_w(woutb, w_out, 8, D, lambda dk: ogcol[:, dk:dk + 1], b_wout)
        b_uvb = B("uvb")
        uvb_ops = []
        for i in range(64):
            r0, r1 = i * 256, (i + 1) * 256
            uvb_ops.append(dma("pool", "uvc", uvb[r0:r1, 0:D], peer_u[r0:r1, :], (), ()))
            uvb_ops.append(dma("pool", "uvc", uvb[r0:r1, D:2 * D], peer_v[r0:r1, :], (), ()))

        def rms_stats(junk_b, x_ap, n, junk_ap, ss_ap, rstd_ap, rd, wr_small):
            sc.op("act", lambda e: e.activation(out=junk_ap, in_=x_ap, func=AF.Square, accum_out=ss_ap), rd, (wr_small, junk_b))
            sc.op("act", lambda e: e.activation(out=ss_ap, in_=ss_ap, func=AF.Sqrt, scale=1.0 / n, bias=EPS), (wr_small,), (wr_small,))
            sc.op("dve", lambda e: e.reciprocal(out=rstd_ap, in_=ss_ap), (wr_small,), (wr_small,))

        def transposes(src_b, nchunk, rd, evac_out, evac_wr, evac_eng="act"):
            for k in range(nchunk):
                sc.op("pe", lambda e, k=k: e.transpose(pst[:, k * 128:(k + 1) * 128], src_b[:, k * 128:(k + 1) * 128], ident[:]),
                      tuple(rd) + (b_consts,), (b_pst,))
            src = pst[:, 0:nchunk * 128].rearrange("p (a b) -> p a b", b=128)
            if evac_eng == "act":
                sc.op("act", lambda e: e.copy(out=evac_out, in_=src), (b_pst,), evac_wr)
            else:
                sc.op("dve", lambda e: e.tensor_copy(out=evac_out, in_=src), (b_pst,), evac_wr)

        b_xnT = sc.bufs_n("xnT", NT)

        def phase_A1(xsrc, xin_, xnb_):
            b_xin = sc.bufs_n("xin", 2); b_xnb = sc.bufs_n("xnb", 2); b_sm = sc.bufs_n("a1s", 2)
            for j in range(NT):
                s_ = j % 2
                dma("sp", f"xin{s_}", xin_[s_], xsrc[j * 128:(j + 1) * 128, :], (), (b_xin[s_],))
                ss = small[:, 2 * s_:2 * s_ + 1]; rs = small[:, 2 * s_ + 1:2 * s_ + 2]
                rms_stats(b_xnb[s_], xin_[s_], D, xnb_[s_], ss, rs, (b_xin[s_],), b_sm[s_])
                sc.op("dve", lambda e, s_=s_, rs=rs: e.tensor_scalar(out=xnb_[s_], in0=xin_[s_], scalar1=rs, scalar2=None, op0=ALU.mult),
                      (b_xin[s_], b_sm[s_]), (b_xnb[s_],))
                transposes(xnb_[s_], 8, (b_xnb[s_],), xnT[:, :, j * 128:(j + 1) * 128], (b_xnT[j],))

        def proj(wb, col0, c0, c1, bk, rd_extra=()):
            tiles = [b_xnT[j] for j in range(c0 // 128, (c1 + 127) // 128)]
            for dk in range(8):
                sc.op("pe", lambda e, dk=dk: e.matmul(bank[bk][:, 0:c1 - c0], wb[:, dk, col0:col0 + 128], xnT[:, dk, c0:c1], start=(dk == 0), stop=(dk == 7)),
                      tuple(tiles) + tuple(rd_extra), (b_bank[bk],))

        def pieces(c0, c1):
            out = []
            c = c0
            while c < c1:
                out.append((c, min(c1, c + 512)))
                c += 512
            return out

        b_wl = B("wlru"); b_gs = B("gset"); b_xr = B("xr"); b_xc = B("xc"); b_xcb = B("xcb"); b_ra = B("ra"); b_iu = B("iu"); b_sh = B("sh")
        b_acc = B("acc4"); b_gg = B("gg")

        def lru_chunk(c, wsets, full, carry_in, carry_out, rev):
            for (a, b_) in pieces(0, SE):
                proj(wlrub, c * 128, a, b_, 0, (b_wl,))
                sc.op("act", lambda e, a=a, b_=b_: e.copy(out=xr_c[:, a:b_], in_=bank[0][:, 0:b_ - a]), (b_bank[0],), (b_xr,))
            if full:
                for (a, b_) in pieces(128, 128 + S):
                    proj(wlrub, 512 + c * 128, a, b_, 1, (b_wl,))
                    sc.op("act", lambda e, a=a, b_=b_: e.activation(out=gg[:, a - 128:b_ - 128], in_=bank[1][:, 0:b_ - a], func=AF.Gelu_apprx_tanh), (b_bank[1],), (b_gg,))
            gv = wsets[0][1]
            sc.op("dve", lambda e: e.tensor_scalar(out=xc_c[:, :], in0=xr_c[:, 126:126 + S], scalar1=gv[:, 16 + c * 5:17 + c * 5], scalar2=gv[:, 12 + c:13 + c], op0=ALU.mult, op1=ALU.add),
                  (b_xr, b_gs), (b_xc,))
            for j in range(1, 5):
                sc.op("dve", lambda e, j=j: e.scalar_tensor_tensor(out=xc_c[:, :], in0=xr_c[:, 126 + j:126 + j + S], scalar=gv[:, 16 + c * 5 + j:17 + c * 5 + j], in1=xc_c[:, :], op0=ALU.mult, op1=ALU.add),
                      (b_xr, b_gs, b_xc), (b_xc,))
            sc.op("act", lambda e: e.copy(out=xcb_c[:, :], in_=xc_c[:, :]), (b_xc,), (b_xcb,))
            for di, (gs_ap, gv_ap, cl_ap, rv) in enumerate(wsets):
                for (a, b_) in pieces(0, S):
                    sc.op("pe", lambda e, a=a, b_=b_, gs_ap=gs_ap: e.matmul(bank[2][:, 0:b_ - a], gs_ap[:, c * 128:(c + 1) * 128], xcb_c[:, a:b_], start=True, stop=True), (b_xcb, b_gs), (b_bank[2],))
                    sc.op("pe", lambda e, a=a, b_=b_, gs_ap=gs_ap: e.matmul(bank[3][:, 0:b_ - a], gs_ap[:, 512 + c * 128:512 + (c + 1) * 128], xcb_c[:, a:b_], start=True, stop=True), (b_xcb, b_gs), (b_bank[3],))
                    sc.op("act", lambda e, a=a, b_=b_, gv_ap=gv_ap: e.activation(out=ra[:, a:b_], in_=bank[2][:, 0:b_ - a], func=AF.Sigmoid, bias=gv_ap[:, c:c + 1]), (b_bank[2], b_gs), (b_ra,))
                    sc.op("act", lambda e, a=a, b_=b_, gv_ap=gv_ap: e.activation(out=iu[:, a:b_], in_=bank[3][:, 0:b_ - a], func=AF.Sigmoid, bias=gv_ap[:, 4 + c:5 + c]), (b_bank[3], b_gs), (b_iu,))
                sc.op("act", lambda e, cl_ap=cl_ap: e.activation(out=ra[:, :], in_=ra[:, :], func=AF.Exp, scale=cl_ap[:, c:c + 1]), (b_ra, b_gs), (b_ra,))
                sc.op("act", lambda e: e.activation(out=sh[:, :], in_=ra[:, :], func=AF.Square), (b_ra,), (b_sh,))
                sc.op("act", lambda e: e.activation(out=sh[:, :], in_=sh[:, :], func=AF.Sqrt, scale=-1.0, bias=1.0), (b_sh,), (b_sh,))
                sc.op("dve", lambda e: e.tensor_tensor(out=iu[:, :], in0=iu[:, :], in1=xc_c[:, :], op=ALU.mult), (b_iu, b_xc), (b_iu,))
                sc.op("dve", lambda e: e.tensor_tensor(out=iu[:, :], in0=iu[:, :], in1=sh[:, :], op=ALU.mult), (b_iu, b_sh), (b_iu,))
                ci = carry_in[di]
                if rv:
                    sc.op("dve", lambda e, ci=ci: e.tensor_tensor_scan(out=apx(sh, [[-1, S]], S - 1), data0=apx(ra, [[-1, S]], S - 1), data1=apx(iu, [[-1, S]], S - 1), initial=carries[:, ci, c:c + 1], op0=ALU.mult, op1=ALU.add),
                          (b_ra, b_iu, b_carr), (b_sh,))
                    last = sh[:, 0:1]
                else:
                    sc.op("dve", lambda e, ci=ci: e.tensor_tensor_scan(out=sh[:, :], data0=ra[:, :], data1=iu[:, :], initial=carries[:, ci, c:c + 1], op0=ALU.mult, op1=ALU.add),
                          (b_ra, b_iu, b_carr), (b_sh,))
                    last = sh[:, S - 1:S]
                co = carry_out[di]
                if co is not None:
                    sc.op("dve", lambda e, co=co, last=last: e.tensor_copy(out=carries[:, co, c:c + 1], in_=last), (b_sh, b_carr), (b_carr,))
                if full:
                    if di == 0:
                        sc.op("dve", lambda e: e.tensor_tensor(out=acc4[:, :], in0=gg[:, :], in1=sh[:, :], op=ALU.mult), (b_gg, b_sh), (b_acc,))
                    else:
                        sc.op("dve", lambda e: e.tensor_tensor(out=gg[:, :], in0=gg[:, :], in1=sh[:, :], op=ALU.mult), (b_gg, b_sh), (b_gg,))
                        sc.op("dve", lambda e: e.tensor_tensor(out=acc4[:, :], in0=acc4[:, :], in1=gg[:, :], op=ALU.add), (b_gg, b_acc), (b_acc,))
            if full:
                sc.op("act", lambda e: e.copy(out=lruT[:, c, :], in_=acc4[:, :]), (b_acc,), (b_lruT[c],))
                sc.op("act", lambda e: e.activation(out=acc4[:, :], in_=acc4[:, :], func=AF.Square), (b_acc,), (b_acc,))
                for j in range(T):
                    sc.op("pe", lambda e, j=j: e.matmul(bank[4][:, j:j + 1], acc4[:, j * 128:(j + 1) * 128], ones_f[:, 0:1], start=True, stop=True), (b_acc, b_c2), (b_bank[4],))
                if c == 0:
                    sc.op("dve", lambda e: e.tensor_copy(out=ssl[:, :], in_=bank[4][:, 0:T]), (b_bank[4],), (b_ssl,))
                else:
                    sc.op("dve", lambda e: e.tensor_tensor(out=ssl[:, :], in0=ssl[:, :], in1=bank[4][:, 0:T], op=ALU.add), (b_bank[4], b_ssl), (b_ssl,))

        def load_wlru():
            load_w(wlrub, w_lru, 8, NLRU, lambda dk: gcol[:, 0, dk:dk + 1], b_wl)

        def light_pass(slot, carry_in_idx, carry_out_idx):
            sc.barrier()
            load_wlru()
            dma("pool", "gsl", gsetl[:, :], gsets[2 + slot, :, 0:1024], (), (b_gs,))
            dma("sp", "gvl", gvecl[:, :], gsets[2 + slot, :, 1024:1064], (), (b_gs,))
            clam_compute(claml[:, :], gvecl[:, 8:12], (b_gs,), (b_gs,))
            phase_A1(x_light[slot], xinL, xnbL)
            for c in range(4):
                lru_chunk(c, [(gsetl, gvecl, claml, False)], False, [carry_in_idx], [carry_out_idx], False)

        for slot in range(6):
            if slot == 0:
                cin = 9
            else:
                sc.op("dve", lambda e, slot=slot: e.tensor_scalar(out=carries[:, 11, :], in0=carries[:, slot - 1, :], scalar1=cfg[:, slot:slot + 1], scalar2=None, op0=ALU.mult), (b_carr, b_consts), (b_carr,))
                cin = 11
            light_pass(slot, cin, slot)
        for (dst, off) in ((7, 8), (8, 16)):
            sc.op("dve", lambda e, dst=dst, off=off: e.tensor_scalar(out=carries[:, dst, :], in0=carries[:, 0, :], scalar1=cfg[:, off:off + 1], scalar2=None, op0=ALU.mult), (b_carr, b_consts), (b_carr,))
            for s_ in range(1, 6):
                sc.op("dve", lambda e, dst=dst, off=off, s_=s_: e.scalar_tensor_tensor(out=carries[:, dst, :], in0=carries[:, s_, :], scalar=cfg[:, off + s_:off + s_ + 1], in1=carries[:, dst, :], op0=ALU.mult, op1=ALU.add), (b_carr, b_consts), (b_carr,))
        light_pass(6, 8, 6)

        b_wq = B("wqkv"); b_rope = B("rope"); b_qr = sc.bufs_n("qr", T); b_kr = sc.bufs_n("kr", NT); b_v = sc.bufs_n("v", NT)
        b_E = sc.bufs_n("E", 3); b_rt = sc.bufs_n("ropet", 2); b_af = B("attnf"); b_ab = B("attnb"); b_bm = B("bmask")

        def full_pass(slot, cf_in, cb_in, cf_out):
            sc.barrier()
            load_w(wqkvb, w_qkv, 8, NQKV, lambda dk: gcol[:, 0, dk:dk + 1], b_wq)
            dma("sp", "rope", cosT[:, :], rope[slot, 0, :, :], (), (b_rope,))
            dma("sp", "rope", sinT[:, :], rope[slot, 1, :, :], (), (b_rope,))
            dma("sp", "bm", bmask[:, :, :], masks[2 + 2 * slot:4 + 2 * slot].rearrange("m p q -> p m q"), (), (b_bm,))
            phase_A1(x_full[slot], xin, xnb)

            def rope_piece(colq, colsw, a, b_, dst, wr):
                proj(wqkvb, colq, a, b_, 0, (b_wq,))
                proj(wqkvb, colsw, a, b_, 1, (b_wq,))
                n = b_ - a
                sc.op("dve", lambda e: e.tensor_tensor(out=ropet[0][:, 0:n], in0=bank[0][:, 0:n], in1=cosT[:, a:b_], op=ALU.mult), (b_bank[0], b_rope), (b_rt[0],))
                sc.op("dve", lambda e: e.tensor_tensor(out=ropet[1][:, 0:n], in0=bank[1][:, 0:n], in1=sinT[:, a:b_], op=ALU.mult), (b_bank[1], b_rope), (b_rt[1],))
                sc.op("dve", lambda e: e.tensor_tensor(out=dst, in0=ropet[0][:, 0:n], in1=ropet[1][:, 0:n], op=ALU.add), (b_rt[0], b_rt[1]), wr)

            for m in range(4):
                for (a, b_) in pieces(128, 128 + S):
                    rope_piece(m * 128, 512 + m * 128, a, b_, qrT[:, m, a - 128:b_ - 128], tuple(b_qr[j] for j in range((a - 128) // 128, (b_ - 128 + 127) // 128)))
            for (a, b_) in pieces(0, SE):
                rope_piece(1024, 1152, a, b_, krT[:, a:b_], tuple(b_kr[j] for j in range(a // 128, (b_ + 127) // 128)))
            for j in range(NT):
                for dk in range(8):
                    sc.op("pe", lambda e, dk=dk, j=j: e.matmul(bank[2][:, 0:128], xnT[:, dk, j * 128:(j + 1) * 128], wqkvb[:, dk, 1280:1408], start=(dk == 0), stop=(dk == 7)), (b_xnT[j], b_wq), (b_bank[2],))
                sc.op("act", lambda e, j=j: e.copy(out=apx(vaug, [[65, 2], [1, 64]], j * 130), in_=bank[2][:, 0:128].rearrange("p (g d) -> p g d", d=64)), (b_bank[2],), (b_v[j],))
                sc.op("dve", lambda e, j=j: e.memset(apx(vaug, [[65, 2], [1, 1]], j * 130 + 64), 1.0), (), (b_v[j],))
            for n in range(T):
                j = n + 1
                for g in range(2):
                    cbs = (j - 1, j, j + 1)
                    for ci, cb in enumerate(cbs):
                        sc.op("pe", lambda e, cb=cb, g=g, n=n, ci=ci: e.matmul(bank[ci][:, :], krT[64 * g:64 * g + 64, cb * 128:(cb + 1) * 128],
                                                                                apx(qrT[64 * g:64 * g + 64, 0, :], [[S, 4], [1, 128]], n * 128), start=True, stop=True),
                              (b_kr[cb], b_qr[n]), (b_bank[ci],))
                        sc.op("act", lambda e, ci=ci: e.activation(out=Et[ci][:, :], in_=bank[ci][:, :], func=AF.Exp, scale=0.125), (b_bank[ci],), (b_E[ci],))
                    for ci, mi in ((0, 0), (2, 1)):
                        if (ci == 0 and n == 0) or (ci == 2 and n == T - 1):
                            msrc = bmask[:, mi, :]; rdm = b_bm
                        else:
                            msrc = mstd[:, mi, :]; rdm = b_consts
                        sc.op("dve", lambda e, ci=ci, msrc=msrc: e.tensor_tensor(out=Et[ci][:, :].rearrange("p (h q) -> p h q", q=128), in0=Et[ci][:, :].rearrange("p (h q) -> p h q", q=128),
                                                                               in1=apx(msrc, [[0, 4], [1, 128]]), op=ALU.mult), (b_E[ci], rdm), (b_E[ci],))
                    ob = 3 + g
                    for hh in range(4):
                        for ci, cb in enumerate(cbs):
                            sc.op("pe", lambda e, hh=hh, ci=ci, cb=cb, g=g, ob=ob: e.matmul(bank[ob][:, hh * 65:(hh + 1) * 65], Et[ci][:, hh * 128:(hh + 1) * 128], vaug[:, cb, g * 65:(g + 1) * 65], start=(ci == 0), stop=(ci == 2)),
                                  (b_E[ci], b_v[cb]), (b_bank[ob],))
                    den = small[:, 8 + 4 * g:12 + 4 * g]
                    sc.op("dve", lambda e, ob=ob, g=g, den=den: e.tensor_tensor(out=den, in0=apx(bank[ob][:, 0:1], [[65, 4]], 64), in1=esink[:, 4 * g:4 * g + 4], op=ALU.add), (b_bank[ob], b_c2), (b_small,))
                    sc.op("dve", lambda e, den=den: e.reciprocal(out=den, in_=den), (b_small,), (b_small,))
                    sc.op("dve", lambda e, ob=ob, g=g, den=den: e.tensor_tensor(out=attnf[:, g * 256:(g + 1) * 256].rearrange("p (h d) -> p h d", d=64), in0=apx(bank[ob][:, 0:1], [[65, 4], [1, 64]]),
                                                                                in1=apx(den, [[1, 4], [0, 64]]), op=ALU.mult), (b_bank[ob], b_small), (b_af,))
                ss = small[:, 16:17]; rs = small[:, 17:18]
                rms_stats(b_ab, attnf[:, :], 512, attnb[:, :], ss, rs, (b_af,), b_small)
                sc.op("dve", lambda e, rs=rs: e.tensor_scalar(out=attnb[:, :], in0=attnf[:, :], scalar1=rs, scalar2=None, op0=ALU.mult), (b_af, b_ab, b_small), (b_ab,))
                transposes(attnb, 4, (b_ab,), attnT[:, :, n * 128:(n + 1) * 128], (b_attnT[n],))
            sc.barrier()
            load_wlru()
            for c in range(4):
                lru_chunk(c, [(gset01[:, 0, :], gvec01[:, 0, :], clam01[:, 0, :], False), (gset01[:, 1, :], gvec01[:, 1, :], clam01[:, 1, :], True)], True,
                          [cf_in, cb_in], [cf_out, None], True)
            sc.op("act", lambda e: e.activation(out=rstdl[:, :], in_=ssl[:, :], func=AF.Sqrt, scale=1.0 / 512, bias=EPS), (b_ssl,), (b_rstdl,))
            sc.op("dve", lambda e: e.reciprocal(out=rstdl[:, :], in_=rstdl[:, :]), (b_rstdl,), (b_rstdl,))
            sc.barrier()
            phase_B(slot)

        b_wpq = B("wpq"); b_keys = B("keys"); b_wg = B("wg"); b_wp = B("wp")
        b_xB = B("xB"); b_h1s = sc.bufs_n("h1", 2); b_xn2 = B("xn2"); b_xn2b = B("xn2b"); b_xn2T = B("xn2T"); b_qpT = B("qpT"); b_subs = B("subs"); b_subt = B("subt")
        b_uv = sc.bufs_n("uvbuf", 8); b_dg = sc.bufs_n("diag", 4); b_dk = sc.bufs_n("dk", 8); b_wk = sc.bufs_n("wk", 8); b_xgs = sc.bufs_n("xn2gb", 2); b_junk = B("junk"); b_junkL = B("junkL")
        b_pBs = sc.bufs_n("pB", 2); b_pBb = B("pBb"); b_pT = B("pT"); b_gate = B("gate")
        b_tk = B("topk"); b_bsm = B("bsmall"); b_eids = sc.bufs_n("eid", 2); b_gsms = sc.bufs_n("gsm", 2)

        def pre_gen(slot, n):
            par = n % 2
            j = n + 1
            h1 = h1s[par]; b_h1 = b_h1s[par]; pB = pBs[par]; b_pB = b_pBs[par]
            xn2gb = xn2gbs[par]; b_xg = b_xgs[par]; eidi = eidis[par]; b_eid = b_eids[par]; gsm = gsms[par]; b_gs_ = b_gsms[par]
            dma("sp", "xB", xB[:, :], x_full[slot, j * 128:(j + 1) * 128, :], (), (b_xB,))
            dma("sp", f"pB{par}", pB[:, :], p_in[slot, n * 128:(n + 1) * 128, :], (), (b_pB,))
            yield
            for half in range(2):
                for m in range(4):
                    sc.op("pe", lambda e, m=m, half=half: e.matmul(bank[half][:, :], attnT[:, m, n * 128:(n + 1) * 128], woutb[:, m, half * 512:(half + 1) * 512], start=(m == 0), stop=(m == 3)),
                          (b_attnT[n], b_wout), (b_bank[half],))
                for c in range(4):
                    sc.op("pe", lambda e, c=c, half=half: e.matmul(bank[2 + half][:, :], lruT[:, c, n * 128:(n + 1) * 128], woutb[:, 4 + c, half * 512:(half + 1) * 512], start=(c == 0), stop=(c == 3)),
                          (b_lruT[c], b_wout), (b_bank[2 + half],))
                hs = slice(half * 512, (half + 1) * 512)
                sc.op("dve", lambda e, half=half, hs=hs: e.tensor_tensor(out=h1[:, hs], in0=bank[half][:, :], in1=xB[:, hs], op=ALU.add), (b_bank[half], b_xB), (b_h1,))
                sc.op("dve", lambda e, half=half, hs=hs: e.scalar_tensor_tensor(out=h1[:, hs], in0=bank[2 + half][:, :], scalar=rstdl[:, n:n + 1], in1=h1[:, hs], op0=ALU.mult, op1=ALU.add),
                      (b_bank[2 + half], b_rstdl, b_h1), (b_h1,))
                yield
            ss = small[:, 20:21]; rs = small[:, 21:22]
            rms_stats(b_junk, h1[:, :], D, junk[:, :], ss, rs, (b_h1,), b_bsm)
            yield
            sc.op("dve", lambda e: e.tensor_scalar(out=xn2b[:, :], in0=h1[:, :], scalar1=rs, scalar2=None, op0=ALU.mult), (b_h1, b_bsm), (b_xn2b,))
            yield
            sc.op("dve", lambda e: e.scalar_tensor_tensor(out=xn2gb[:, :], in0=h1[:, :], scalar=rs, in1=grow[:, 0, :], op0=ALU.mult, op1=ALU.mult), (b_h1, b_bsm, b_consts), (b_xg,))
            yield
            transposes(xn2b, 8, (b_xn2b,), xn2T[:, :, :], (b_xn2T,))
            yield
            for grp in range(4):
                for cc in range(4):
                    ch = grp * 4 + cc
                    for dk in range(8):
                        sc.op("pe", lambda e, dk=dk, ch=ch, cc=cc, grp=grp: e.matmul(bank[grp][:, cc * 128:(cc + 1) * 128], wpqb[:, dk, ch * 128:(ch + 1) * 128], xn2T[:, dk, :], start=(dk == 0), stop=(dk == 7)),
                              (b_xn2T, b_wpq), (b_bank[grp],))
                sc.op("act", lambda e, grp=grp: e.copy(out=qpT[:, grp * 4:(grp + 1) * 4, :], in_=bank[grp][:, :].rearrange("p (a b) -> p a b", b=128)), (b_bank[grp],), (b_qpT,))
                yield
            for grp in range(4):
                for cc in range(4):
                    ch = grp * 4 + cc
                    sc.op("pe", lambda e, ch=ch, cc=cc, grp=grp: e.matmul(bank[grp][:, cc * 128:(cc + 1) * 128], qpT[:, ch, :], keysb[:, ch, :], start=True, stop=True), (b_qpT, b_keys), (b_bank[grp],))
                sc.op("act", lambda e, grp=grp: e.copy(out=subs[:, grp * 4:(grp + 1) * 4, :], in_=bank[grp][:, :].rearrange("p (a b) -> p a b", b=128)), (b_bank[grp],), (b_subs,))
                yield
            for ch in range(16):
                sc.op("dve", lambda e, ch=ch: e.max(out=top0[:, ch, 0:8], in_=subs[:, ch, :]), (b_subs,), (b_tk,))
                sc.op("dve", lambda e, ch=ch: e.max_index(out=idxu[:, ch, 0:8], in_max=top0[:, ch, 0:8], in_values=subs[:, ch, :]), (b_subs, b_tk), (b_tk,))
                sc.op("dve", lambda e, ch=ch: e.match_replace(out=subt[:, :], in_to_replace=top0[:, ch, 0:8], in_values=subs[:, ch, :], imm_value=-1e30), (b_subs, b_tk), (b_subt,))
                sc.op("dve", lambda e, ch=ch: e.max(out=top0[:, ch, 8:16], in_=subt[:, :]), (b_subt,), (b_tk,))
                sc.op("dve", lambda e, ch=ch: e.max_index(out=idxu[:, ch, 8:16], in_max=top0[:, ch, 8:16], in_values=subt[:, :]), (b_subt, b_tk), (b_tk,))
                yield
            sc.op("dve", lambda e: e.tensor_copy(out=idxf[:, :, :], in_=idxu[:, :, :]), (b_tk,), (b_tk,))
            sc.op("dve", lambda e: e.tensor_tensor(out=cand[:, :, :].rearrange("p h (a b) -> p h a b", b=16), in0=apx(top0[:, 0, :], [[32, 8], [1, 16], [0, 16]]), in1=apx(top0[:, 0, :], [[32, 8], [0, 16], [1, 16]], 16), op=ALU.add),
                  (b_tk,), (b_tk, b_subs))
            yield
            for h in range(8):
                sc.op("dve", lambda e, h=h: e.max(out=cbest[:, h, 0:8], in_=cand[:, h, :]), (b_tk,), (b_tk,))
                sc.op("dve", lambda e, h=h: e.max_index(out=posu[:, h, 0:8], in_max=cbest[:, h, 0:8], in_values=cand[:, h, :]), (b_tk,), (b_tk,))
                sc.op("dve", lambda e, h=h: e.match_replace(out=candt[:, :], in_to_replace=cbest[:, h, 0:8], in_values=cand[:, h, :], imm_value=-1e30), (b_tk,), (b_subt,))
                sc.op("dve", lambda e, h=h: e.max(out=cbest[:, h, 8:16], in_=candt[:, :]), (b_subt,), (b_tk,))
                sc.op("dve", lambda e, h=h: e.max_index(out=posu[:, h, 8:16], in_max=cbest[:, h, 8:16], in_values=candt[:, :]), (b_subt, b_tk), (b_tk,))
                yield
            sc.op("dve", lambda e: e.tensor_single_scalar(out=posf[:, :, :].bitcast(U32), in_=posu[:, :, :], scalar=4, op=ALU.logical_shift_right), (b_tk,), (b_tk,))
            sc.op("dve", lambda e: e.tensor_copy(out=afl[:, :, :], in_=posf[:, :, :].bitcast(U32)), (b_tk,), (b_tk,))
            sc.op("dve", lambda e: e.tensor_single_scalar(out=posf[:, :, :].bitcast(U32), in_=posu[:, :, :], scalar=15, op=ALU.bitwise_and), (b_tk,), (b_tk,))
            sc.op("dve", lambda e: e.tensor_copy(out=bfl[:, :, :], in_=posf[:, :, :].bitcast(U32)), (b_tk,), (b_tk,))
            yield
            for (sel, which, dst) in ((afl, 0, isel), (bfl, 1, jsel)):
                sc.op("dve", lambda e, sel=sel: e.tensor_tensor(out=oh[:, :, :], in0=apx(sel[:, 0, :], [[1, 128], [0, 16]]), in1=apx(iota16[:, :], [[0, 128], [1, 16]]), op=ALU.is_equal), (b_tk, b_consts), (b_tk, b_subs))
                yield
                sc.op("dve", lambda e, which=which: e.tensor_tensor(out=oh[:, :, :].rearrange("p (h k) a -> p h k a", k=16), in0=oh[:, :, :].rearrange("p (h k) a -> p h k a", k=16),
                                                                      in1=apx(idxf[:, 0, :], [[32, 8], [0, 16], [1, 16]], 16 * which), op=ALU.mult), (b_tk,), (b_tk,))
                yield
                sc.op("dve", lambda e, dst=dst: e.tensor_reduce(out=dst[:, :], in_=oh[:, :, :], axis=AX.X, op=ALU.add), (b_tk,), (b_tk,))
                yield
            sc.op("dve", lambda e: e.scalar_tensor_tensor(out=eidf[:, :], in0=isel[:, :], scalar=128.0, in1=jsel[:, :], op0=ALU.mult, op1=ALU.add), (b_tk,), (b_tk,))
            sc.op("dve", lambda e: e.tensor_copy(out=eidi[:, :], in_=eidf[:, :]), (b_tk,), (b_eid,))
            yield
            sc.op("dve", lambda e: e.tensor_tensor(out=gsm[:, :, :], in0=cbest[:, :, :], in1=apx(cbest[:, 0, :], [[16, 8], [0, 16]]), op=ALU.subtract), (b_tk,), (b_gs_,))
            sc.op("act", lambda e: e.activation(out=gsm[:, :, :], in_=gsm[:, :, :], func=AF.Exp), (b_gs_,), (b_gs_,))
            sc.op("dve", lambda e: e.tensor_reduce(out=small[:, 24:32], in_=gsm[:, :, :], axis=AX.X, op=ALU.add), (b_gs_,), (b_bsm,))
            sc.op("dve", lambda e: e.reciprocal(out=small[:, 24:32], in_=small[:, 24:32]), (b_bsm,), (b_bsm,))
            sc.op("dve", lambda e: e.tensor_tensor(out=gsm[:, :, :], in0=gsm[:, :, :], in1=apx(small[:, 24:32], [[1, 8], [0, 16]]), op=ALU.mult), (b_gs_, b_bsm), (b_gs_,))
            yield

        def loop_front(n, kk):
            par = n % 2
            xn2gb = xn2gbs[par]; b_xg = b_xgs[par]; eidi = eidis[par]; b_eid = b_eids[par]
            s_ = kk % NSLOT; s8 = kk % 8
            o = sc.op("pool", lambda e: e.indirect_dma_start(out=uvbuf[s_][:, :], out_offset=None, in_=uvb[:, :], in_offset=bass.IndirectOffsetOnAxis(ap=eidi[:, kk:kk + 1], axis=0)),
                      (b_eid,), (b_uv[s_],) + ((b_stage,) if s_ >= 6 else ()), dma=f"ug{s_}")
            if uvb_ops:
                o.deps.extend(uvb_ops)
                uvb_ops.clear()
            sc.op("dve", lambda e: e.scalar_tensor_tensor(out=junkL[:, :], in0=uvbuf[s_][:, 0:D], scalar=1.0, in1=xn2gb[:, :], op0=ALU.mult, op1=ALU.mult, accum_out=dots[:, kk:kk + 1]),
                  (b_uv[s_], b_xg), (b_junkL, b_dk[s8]))
            sc.op("act", lambda e: e.activation(out=wts[:, kk:kk + 1], in_=dots[:, kk:kk + 1], func=AF.Gelu_apprx_tanh), (b_dk[s8],), (b_wk[s8],))

        def loop_back(n, kk):
            par = n % 2
            gsm = gsms[par]; b_gs_ = b_gsms[par]
            gflat = gsm[:, :, :].rearrange("p h k -> p (h k)")
            s_ = kk % NSLOT; s2 = kk % 4; s8 = kk % 8
            sc.op("dve", lambda e: e.tensor_scalar(out=diag[s2][:, :], in0=ident[:, :], scalar1=wts[:, kk:kk + 1], scalar2=gflat[:, kk:kk + 1], op0=ALU.mult, op1=ALU.mult),
                  (b_wk[s8], b_gs_, b_consts), (b_dg[s2],))
            for half in range(2):
                sc.op("pe", lambda e, half=half: e.matmul(bank[5 + half][:, :], diag[s2][:, :], uvbuf[s_][:, D + half * 512:D + (half + 1) * 512], start=(kk == 0), stop=(kk == 127)),
                      (b_dg[s2], b_uv[s_]), (b_bank[5 + half],))

        def post_gen(slot, n):
            par = n % 2
            h1 = h1s[par]; b_h1 = b_h1s[par]; pB = pBs[par]; b_pB = b_pBs[par]
            ss = small[:, 22:23]; rs = small[:, 23:24]
            rms_stats(b_junk, h1[:, :], D, junk[:, :], ss, rs, (b_h1,), b_bsm)
            yield
            sc.op("dve", lambda e: e.tensor_scalar(out=xn2b[:, :], in0=h1[:, :], scalar1=rs, scalar2=None, op0=ALU.mult), (b_h1, b_bsm), (b_xn2b,))
            yield
            transposes(xn2b, 8, (b_xn2b,), xn2T[:, :, :], (b_xn2T,))
            yield
            sc.op("dve", lambda e: e.tensor_copy(out=pBb[:, :], in_=pB[:, :]), (b_pB,), (b_pBb,))
            transposes(pBb, 2, (b_pBb,), pT[:, :, :], (b_pT,))
            yield
            for half in range(2):
                for dk in range(8):
                    sc.op("pe", lambda e, dk=dk, half=half: e.matmul(bank[half][:, :], xn2T[:, dk, :], wgb[:, dk, half * 512:(half + 1) * 512], start=(dk == 0), stop=(dk == 7)), (b_xn2T, b_wg), (b_bank[half],))
                for kc in range(2):
                    sc.op("pe", lambda e, kc=kc, half=half: e.matmul(bank[2 + half][:, :], pT[:, kc, :], wpb[:, kc, half * 512:(half + 1) * 512], start=(kc == 0), stop=(kc == 1)), (b_pT, b_wp), (b_bank[2 + half],))
                hs = slice(half * 512, (half + 1) * 512)
                sc.op("act", lambda e, half=half, hs=hs: e.activation(out=gate[:, hs], in_=bank[half][:, :], func=AF.Sigmoid), (b_bank[half],), (b_gate,))
                yield
                sc.op("dve", lambda e, half=half, hs=hs: e.tensor_tensor(out=gate[:, hs], in0=gate[:, hs], in1=bank[2 + half][:, :], op=ALU.mult), (b_gate, b_bank[2 + half]), (b_gate,))
                yield
            sc.op("dve", lambda e: e.tensor_tensor(out=h1[:, :], in0=h1[:, :], in1=gate[:, :], op=ALU.add), (b_h1, b_gate), (b_h1,))
            yield
            ss2 = small[:, 32:33]; rs2 = small[:, 33:34]
            rms_stats(b_junk, h1[:, :], D, junk[:, :], ss2, rs2, (b_h1,), b_bsm)
            yield
            sc.op("dve", lambda e: e.scalar_tensor_tensor(out=gate[:, :], in0=h1[:, :], scalar=rs2, in1=grow[:, 1, :], op0=ALU.mult, op1=ALU.mult), (b_h1, b_bsm, b_consts), (b_gate,))
            dma("sp", "yout", y_out[slot, n * 128:(n + 1) * 128, :], gate[:, :], (b_gate,), ())
            yield

        def phase_B(slot):
            load_w(wpqb, w_pq, 8, 2048, lambda dk: gcol[:, 1, dk:dk + 1], b_wpq)
            load_w(wgb, w_gate, 8, D, lambda dk: gcol[:, 2, dk:dk + 1], b_wg)
            load_w(wpb, w_proj, 2, D, None, b_wp)
            dma("pool", "keys", keysb[:, :, :], keysT[:, :, :], (), (b_keys,))
            for _ in pre_gen(slot, 0):
                pass
            prev_post = None
            for n in range(T):
                gens = []
                if prev_post is not None:
                    gens.append(prev_post)
                if n + 1 < T:
                    gens.append(pre_gen(slot, n + 1))

                def side():
                    for g in gens:
                        yield from g
                side_it = side()
                LAG = 1
                for kk in range(128 + LAG):
                    if kk < 128:
                        loop_front(n, kk)
                    if kk >= LAG:
                        loop_back(n, kk - LAG)
                    next(side_it, None)
                par = n % 2
                for half in range(2):
                    hs = slice(half * 512, (half + 1) * 512)
                    sc.op("dve", lambda e, half=half, hs=hs, par=par: e.tensor_tensor(out=h1s[par][:, hs], in0=h1s[par][:, hs], in1=bank[5 + half][:, :], op=ALU.add), (b_h1s[par], b_bank[5 + half]), (b_h1s[par],))
                for _ in side_it:
                    pass
                prev_post = post_gen(slot, n)
            for _ in prev_post:
                pass

        for slot in range(4):
            full_pass(slot, 9, 9, None)
        full_pass(4, 7, 6, 10)
        full_pass(5, 10, 8, None)

        sc.emit(nc, stack)
    return nc, sc


def _cols(a):
    return np.ascontiguousarray(a.reshape(-1, 128).T)


def _prep_shared(inp, T):
    f = np.float32
    w_in = inp["w_in"][0]
    qcols = np.zeros(512, np.int64); qsw = np.zeros(512, np.int64)
    for m in range(4):
        for p in range(128):
            head = m + 4 * (p // 64); d = p % 64
            qcols[m * 128 + p] = head * 64 + d
            qsw[m * 128 + p] = head * 64 + (d + 32) % 64
    kcols = 512 + np.arange(128)
    ksw = 512 + (np.arange(128) // 64) * 64 + (np.arange(128) % 64 + 32) % 64
    vcols = 640 + np.arange(128)
    w_qkv = np.ascontiguousarray(w_in[:, np.concatenate([qcols, qsw, kcols, ksw, vcols])])
    w_lru = np.ascontiguousarray(w_in[:, 768:1792])
    g_cols = np.zeros((128, 4, 8), f)
    g_cols[:, 0] = _cols(inp["mix_norm_g"][0]); g_cols[:, 1] = _cols(inp["ffn_norm_g"][0]); g_cols[:, 2] = _cols(inp["ple_norm_g"][0])
    g_rows = np.stack([inp["ffn_norm_g"][0], inp["final_norm_g"]]).astype(f)
    og_cols = np.concatenate([_cols(inp["attn_out_norm_g"][0]), _cols(inp["lru_out_norm_g"][0])], axis=1).astype(f)
    sink = inp["attn_sink"].reshape(1, 8).astype(f)
    keysT = np.ascontiguousarray(inp["peer_keys"][0].reshape(16, 128, 128).transpose(2, 0, 1))
    ident = np.eye(128, dtype=f).astype(ml_dtypes.bfloat16)
    iota16 = np.tile(np.arange(16, dtype=f)[None, :], (128, 1))

    def gset(direction, flip):
        g = np.zeros((128, 8 * 128 + 40), f)
        for wi, wname in enumerate(("lru_wa", "lru_wx")):
            w = inp[wname][0, direction]
            for c in range(4):
                for bb in range(2):
                    blk = 2 * c + bb
                    g[bb * 64:(bb + 1) * 64, wi * 512 + c * 128 + bb * 64: wi * 512 + c * 128 + (bb + 1) * 64] = w[blk]
        g[:, 1024:1028] = _cols(inp["lru_ba"][0, direction]); g[:, 1028:1032] = _cols(inp["lru_bx"][0, direction])
        g[:, 1032:1036] = _cols(inp["lru_lambda"][0, direction]); g[:, 1036:1040] = _cols(inp["conv_b"][0])
        cw = inp["conv_w"][0]
        w5 = np.zeros((5, 512), f)
        if flip:
            w5[1:5] = cw[::-1]
        else:
            w5[0:4] = cw
        for c in range(4):
            g[:, 1040 + c * 5:1040 + (c + 1) * 5] = w5[:, c * 128:(c + 1) * 128].T
        return g
    return dict(w_qkv=w_qkv, w_lru=w_lru, g_cols=g_cols, g_rows=g_rows, og_cols=og_cols, sink=sink, keysT=keysT, ident=ident,
                iota16=iota16, w_out=np.ascontiguousarray(inp["w_out"][0]), w_pq=np.ascontiguousarray(inp["peer_wq"][0]),
                peer_u=inp["peer_u"][0], peer_v=inp["peer_v"][0], w_gate=np.ascontiguousarray(inp["ple_w_gate"][0]),
                w_proj=np.ascontiguousarray(inp["ple_w_proj"][0])), gset


def _seg(xseq, start, S):
    n = xseq.shape[0]
    out = np.zeros((S + 256, xseq.shape[1]), np.float32)
    lo = max(0, start - 128); hi = min(n, start + S + 128)
    out[lo - (start - 128):hi - (start - 128)] = xseq[lo:hi]
    return out


def _rope_tab(start, SE):
    pos = (np.arange(SE, dtype=np.float32) + np.float32(start - 128)).astype(np.float32)
    half = 32
    inv = (np.float32(10000.0) ** (-np.arange(half, dtype=np.float32) / np.float32(half))).astype(np.float32)
    ang = (pos[None, :] * inv[:, None]).astype(np.float32)
    c = np.cos(ang).astype(np.float32); s_ = np.sin(ang).astype(np.float32)
    cosT = np.zeros((128, SE), np.float32); sinT = np.zeros((128, SE), np.float32)
    for p in range(128):
        d = p % 64
        cosT[p] = c[d % 32]
        sinT[p] = -s_[d % 32] if d < 32 else s_[d % 32]
    return np.stack([cosT, sinT])


def kernel(**inp):
    inp = {k: np.asarray(v) for k, v in inp.items()}
    xp = inp["x_prompt"]; xs = inp["x_sample"]; pp = inp["p_prompt"][0]; ps = inp["p_sample"][0]
    S = xp.shape[1]; T = S // 128; SE = S + 256
    assert xp.shape[0] == 32 and xs.shape[0] == 2 and xs.shape[1] == 8 * S
    shared, gset = _prep_shared(inp, T)
    cq = np.arange(128)[:, None]; qq = np.arange(128)[None, :]
    m_prev = (cq >= qq).astype(np.float32); m_next = (cq <= qq).astype(np.float32); m_zero = np.zeros((128, 128), np.float32)
    rope_p = _rope_tab(0, SE)
    in_maps = []
    for c in range(NCORES):
        sq, q = c // 4, c % 4
        xseq = xs[sq]; xrev = xseq[::-1]
        t0 = q * 2 * S
        x_full = np.stack([_seg(xp[4 * c + k], 0, S) for k in range(4)] + [_seg(xseq, t0, S), _seg(xseq, t0 + S, S)])
        p_in = np.stack([pp[4 * c + k] for k in range(4)] + [ps[sq, t0:t0 + S], ps[sq, t0 + S:t0 + 2 * S]])
        nF = 2 * q; nB = 6 - 2 * q
        lights = [_seg(xseq, s_ * S, S) for s_ in range(nF)] + [_seg(xrev, r * S, S) for r in range(nB)] + [_seg(xrev, nB * S, S)]
        x_light = np.stack(lights)
        gs = [gset(0, False), gset(1, False)] + [gset(0, False)] * nF + [gset(1, True)] * nB + [gset(1, True)]
        cfg = np.zeros((128, 32), np.float32)
        for s_ in range(6):
            cfg[:, s_] = 0.0 if (s_ == 0 or s_ == nF) else 1.0
        if nF > 0:
            cfg[:, 8 + nF - 1] = 1.0
        if nB > 0:
            cfg[:, 16 + 5] = 1.0
        rope = np.stack([rope_p] * 4 + [_rope_tab(t0, SE), _rope_tab(t0 + S, SE)])
        mk = [m_prev, m_next]
        for k in range(4):
            mk += [m_zero, m_zero]
        mk += [m_prev if q > 0 else m_zero, m_prev]
        mk += [m_prev, m_next if q < 3 else m_zero]
        mk[2 + 2 * 4 + 1] = m_next
        masks = np.stack(mk).astype(ml_dtypes.bfloat16)
        m = dict(shared)
        m.update(x_full=x_full, x_light=x_light, p_in=np.ascontiguousarray(p_in), gsets=np.stack(gs), cfg=cfg, rope=rope, masks=masks)
        in_maps.append(m)
    nc, sc = build_program(T)
    res = run_bass_kernel_spmd(nc, in_maps, core_ids=list(range(NCORES)))
    y_prompt = np.zeros(xp.shape, np.float32); y_sample = np.zeros(xs.shape, np.float32)
    for c in range(NCORES):
        y = res.results[c]["y_out"]
        sq, q = c // 4, c % 4
        t0 = q * 2 * S
        for k in range(4):
            y_prompt[4 * c + k] = y[k]
        y_sample[sq, t0:t0 + S] = y[4]
        y_sample[sq, t0 + S:t0 + 2 * S] = y[5]
    return (y_prompt, y_sample)
```

```python
from contextlib import ExitStack
import numpy as np
import ml_dtypes
import concourse.bass as bass
import concourse.mybir as mybir
from concourse.bass_utils import run_bass_kernel_spmd

F32 = mybir.dt.float32
BF16 = mybir.dt.bfloat16
I32 = mybir.dt.int32
U32 = mybir.dt.uint32
AF = mybir.ActivationFunctionType
ALU = mybir.AluOpType
AX = mybir.AxisListType

D = 1024
NCORES = 8
EPS = 1e-6
NQKV = 1408
NLRU = 1024


class Buf:
    __slots__ = ("name", "w", "r")

    def __init__(self, name):
        self.name = name
        self.w = None
        self.r = []


class Op:
    __slots__ = ("eng", "fn", "deps", "dma", "sig", "needed", "idx")

    def __init__(self, eng, fn, dma):
        self.eng = eng
        self.fn = fn
        self.deps = []
        self.dma = dma
        self.sig = None
        self.needed = False


class Sched:
    ENGS = ("pe", "act", "dve", "pool", "sp")

    def __init__(self):
        self.ops = []
        self.bufs = []
        self.bar = []
        self.last = {}
        self.dma_last = {}

    def buf(self, name):
        b = Buf(name)
        self.bufs.append(b)
        return b

    def bufs_n(self, name, n):
        return [self.buf(f"{name}{i}") for i in range(n)]

    def op(self, eng, fn, reads=(), writes=(), dma=None):
        o = Op(eng, fn, dma)
        deps = list(self.bar)
        for b in reads:
            if b.w is not None:
                deps.append(b.w)
        for b in writes:
            if b.w is not None:
                deps.append(b.w)
            deps.extend(b.r)
        o.deps = deps
        for b in reads:
            b.r.append(o)
        for b in writes:
            b.w = o
            b.r = []
        self.ops.append(o)
        self.last[eng] = o
        if dma is not None:
            self.dma_last[dma] = o
        return o

    def barrier(self):
        self.bar = list(self.last.values()) + list(self.dma_last.values())
        for b in self.bufs:
            b.w = None
            b.r = []

    def emit(self, nc, stack, final_eng="sp"):
        fin = Op(final_eng, None, None)
        fin.deps = list(self.last.values()) + list(self.dma_last.values())
        self.ops.append(fin)
        for o in self.ops:
            for d in o.deps:
                d.needed = True
        MAXC = 30000
        eng_sems = {e: [] for e in self.ENGS}
        eng_cnt = {e: MAXC for e in self.ENGS}
        dma_sems = {}
        dma_cnt = {}
        nsem = [0]

        def newsem(nm):
            nsem[0] += 1
            return stack.enter_context(nc.semaphore(f"{nm}_{nsem[0]}"))

        for o in self.ops:
            if o.dma is not None:
                if o.dma not in dma_sems or dma_cnt[o.dma] >= 32000:
                    dma_sems[o.dma] = newsem("d")
                    dma_cnt[o.dma] = 0
                dma_cnt[o.dma] += 16
                o.sig = (dma_sems[o.dma], dma_cnt[o.dma])
            elif o.needed:
                if eng_cnt[o.eng] >= MAXC:
                    eng_sems[o.eng].append(newsem(o.eng))
                    eng_cnt[o.eng] = 0
                eng_cnt[o.eng] += 1
                o.sig = (eng_sems[o.eng][-1], eng_cnt[o.eng])
        per_eng = {e: [] for e in self.ENGS}
        for o in self.ops:
            per_eng[o.eng].append(o)
        self.n_ops = len(self.ops)

        def run(eng_name, eng):
            waited = {}
            for o in per_eng[eng_name]:
                for d in o.deps:
                    if d.dma is None and d.eng == "pe" and eng_name == "pe":
                        continue
                    sem, val = d.sig
                    k = id(sem)
                    if waited.get(k, 0) >= val:
                        continue
                    waited[k] = val
                    eng.wait_ge(sem, val)
                if o.fn is None:
                    continue
                inst = o.fn(eng)
                if o.dma is not None:
                    inst.then_inc(o.sig[0], 16)
                elif o.sig is not None:
                    inst.then_inc(o.sig[0], 1)

        with nc.Block() as block:
            @block.sync
            def _(e):
                run("sp", e)

            @block.scalar
            def _(e):
                run("act", e)

            @block.vector
            def _(e):
                run("dve", e)

            @block.gpsimd
            def _(e):
                run("pool", e)

            @block.tensor
            def _(e):
                run("pe", e)


def apx(base, dims, off=0):
    return bass.AP(tensor=base.tensor, offset=base.offset + off, ap=[list(base.ap[0])] + [list(d) for d in dims])


def build_program(T):
    NT = T + 2
    S = T * 128
    SE = NT * 128
    NFULL = 6
    NLIGHT = 7
    NPIECE = (S + 511) // 512
    PW = min(512, S)
    nc = bass.Bass("TRN2", target_bir_lowering=False)
    dt_in = lambda n, s, d=F32: nc.dram_tensor(n, s, d, kind="ExternalInput").ap()
    x_full = dt_in("x_full", [NFULL, SE, D])
    x_light = dt_in("x_light", [NLIGHT, SE, D])
    p_in = dt_in("p_in", [NFULL, S, 256])
    w_qkv = dt_in("w_qkv", [D, NQKV])
    w_lru = dt_in("w_lru", [D, NLRU])
    g_cols = dt_in("g_cols", [128, 4, 8])
    g_rows = dt_in("g_rows", [2, D])
    og_cols = dt_in("og_cols", [128, 8])
    sink = dt_in("sink", [1, 8])
    gsets = dt_in("gsets", [2 + NLIGHT, 128, 8 * 128 + 40])
    w_out = dt_in("w_out", [D, D])
    w_pq = dt_in("w_pq", [D, 2048])
    keysT = dt_in("keysT", [128, 16, 128])
    peer_u = dt_in("peer_u", [16384, D])
    peer_v = dt_in("peer_v", [16384, D])
    w_gate = dt_in("w_gate", [D, D])
    w_proj = dt_in("w_proj", [256, D])
    rope = dt_in("rope", [NFULL, 2, 128, SE])
    masks = dt_in("masks", [2 + 2 * NFULL, 128, 128], BF16)
    ident_in = dt_in("ident", [128, 128], BF16)
    cfg_in = dt_in("cfg", [128, 32])
    iota_in = dt_in("iota16", [128, 16])
    y_out = nc.dram_tensor("y_out", [NFULL, S, D], F32, kind="ExternalOutput").ap()
    uvb = nc.dram_tensor("uvb", [16384, 2 * D], BF16, kind="Internal").ap()

    sc = Sched()
    stack = ExitStack()
    with stack:
        sb = lambda n, s, d=F32: stack.enter_context(nc.sbuf_tensor(n, s, d))
        ident = sb("ident_sb", [128, 128], BF16)
        woutb = sb("woutb", [128, 8, D], BF16)
        lruT = sb("lruT", [128, 4, S], BF16)
        attnT = sb("attnT", [128, 4, S], BF16)
        grow = sb("grow", [128, 2, D])
        gcol = sb("gcol", [128, 4, 8])
        ogcol = sb("ogcol", [128, 8])
        esink = sb("esink", [128, 8])
        mstd = sb("mstd", [128, 2, 128], BF16)
        cfg = sb("cfg_sb", [128, 32])
        iota16 = sb("iota_sb", [128, 16])
        gset01 = sb("gset01", [128, 2, 8 * 128], BF16)
        gvec01 = sb("gvec01", [128, 2, 40])
        clam01 = sb("clam01", [128, 2, 4])
        rstdl = sb("rstdl", [128, T])
        ssl = sb("ssl", [128, T])
        ones_f = sb("ones_f", [128, 1])
        carries = sb("carries", [128, 16, 4])
        small = sb("small", [128, 64])
        stage = sb("stage", [128, 2048])
        RB = 70400
        R = sb("R", [128, RB], BF16)
        cur = [0]

        def carve(shape, dt):
            n = int(np.prod(shape))
            units = n * (2 if dt in (F32, I32, U32) else 1)
            a = R[:, cur[0]:cur[0] + units]
            cur[0] += units + (units % 2)
            assert cur[0] <= RB, (cur[0], RB)
            if dt != BF16:
                a = a.bitcast(dt)
            if len(shape) == 2:
                return a.rearrange("p (a b) -> p a b", b=shape[1])
            if len(shape) == 3:
                return a.rearrange("p (a b c) -> p a b c", b=shape[1], c=shape[2])
            return a

        xnT = carve([8, SE], BF16)
        markX = cur[0]
        wqkvb = carve([8, NQKV], BF16)
        cosT = carve([SE], F32)
        sinT = carve([SE], F32)
        qrT = carve([4, S], BF16)
        krT = carve([SE], BF16)
        vaug = carve([NT, 130], BF16)
        Et = [carve([512], BF16) for _ in range(3)]
        ropet = [carve([512], F32) for _ in range(2)]
        attnf = carve([512], F32)
        attnb = carve([512], BF16)
        bmask = carve([2, 128], BF16)
        xin = [carve([D], F32) for _ in range(2)]
        xnb = [carve([D], BF16) for _ in range(2)]
        endA2 = cur[0]
        cur[0] = markX
        wlrub = carve([8, NLRU], BF16)
        xr_c = carve([SE], F32)
        xc_c = carve([S], F32)
        xcb_c = carve([S], BF16)
        ra = carve([S], F32)
        iu = carve([S], F32)
        sh = carve([S], F32)
        acc4 = carve([S], F32)
        gg = carve([S], F32)
        gsetl = carve([8 * 128], BF16)
        gvecl = carve([40], F32)
        claml = carve([4], F32)
        xinL = [carve([D], F32) for _ in range(2)]
        xnbL = [carve([D], BF16) for _ in range(2)]
        endLRU = cur[0]
        cur[0] = 0
        wpqb = carve([8, 2048], BF16)
        keysb = carve([16, 128], BF16)
        wgb = carve([8, D], BF16)
        wpb = carve([2, D], BF16)
        xB = carve([D], F32)
        h1s = [carve([D], F32) for _ in range(2)]
        xn2b = carve([D], BF16)
        xn2T = carve([8, 128], BF16)
        qpT = carve([16, 128], BF16)
        subs = carve([16, 128], F32)
        subt = carve([128], F32)
        stage_b = stage[:, :].bitcast(BF16)
        uvbuf = [carve([2 * D], BF16) for _ in range(6)] + [stage_b[:, 0:2 * D], stage_b[:, 2 * D:4 * D]]
        NSLOT = 8
        xn2gbs = [carve([D], BF16) for _ in range(2)]
        junkL = carve([D], BF16)
        diag = [carve([128], BF16) for _ in range(4)]
        junk = carve([D], BF16)
        pBs = [carve([256], F32) for _ in range(2)]
        pBb = carve([256], BF16)
        pT = carve([2, 128], BF16)
        gate = carve([D], F32)
        top0 = carve([16, 16], F32)
        idxu = carve([16, 16], U32)
        idxf = carve([16, 16], F32)
        cand = subs.rearrange("p a b -> p (a b)").rearrange("p (h c) -> p h c", c=256)
        candt = carve([256], F32)
        cbest = carve([8, 16], F32)
        posu = carve([8, 16], U32)
        posf = carve([8, 16], F32)
        afl = carve([8, 16], F32)
        bfl = carve([8, 16], F32)
        oh = subs.rearrange("p a b -> p (a b)").rearrange("p (k a) -> p k a", a=16)
        isel = carve([128], F32)
        jsel = carve([128], F32)
        eidf = carve([128], F32)
        eidis = [carve([128], I32) for _ in range(2)]
        gsms = [carve([8, 16], F32) for _ in range(2)]
        dots = carve([128], F32)
        wts = carve([128], F32)
        endB = cur[0]

        pst = stack.enter_context(nc.psum_tensor("pst", [128, 1024], BF16))
        bank = [stack.enter_context(nc.psum_tensor(f"bank{i}", [128, 512], F32)) for i in range(7)]

        B = sc.buf
        b_pst = B("pst")
        b_bank = sc.bufs_n("bank", 7)
        b_stage = B("stage")
        b_wout = B("wout"); b_lruT = [B(f"lruT{c}") for c in range(4)]; b_attnT = [B(f"attnT{j}") for j in range(T)]
        b_consts = B("consts")
        b_small = B("small")
        b_carr = B("carries")
        b_rstdl = B("rstdl"); b_ssl = B("ssl")

        def dma(eng, key, out, in_, reads=(), writes=()):
            return sc.op(eng, lambda e, out=out, in_=in_: e.dma_start(out=out, in_=in_), reads, writes, dma=key)

        dma("sp", "c0", ident[:], ident_in[:, :], (), (b_consts,))
        dma("sp", "c0", grow[:, 0, :], g_rows[0:1, :].partition_broadcast(128), (), (b_consts,))
        dma("sp", "c0", grow[:, 1, :], g_rows[1:2, :].partition_broadcast(128), (), (b_consts,))
        dma("sp", "c0", gcol[:], g_cols[:, :, :], (), (b_consts,))
        dma("sp", "c0", ogcol[:], og_cols[:, :], (), (b_consts,))
        dma("sp", "c0", esink[:], sink[0:1, :].partition_broadcast(128), (), (b_consts,))
        dma("sp", "c0", mstd[:], masks[0:2].rearrange("m p q -> p m q"), (), (b_consts,))
        dma("sp", "c0", cfg[:], cfg_in[:, :], (), (b_consts,))
        dma("sp", "c0", iota16[:], iota_in[:, :], (), (b_consts,))
        for d_ in range(2):
            dma("pool", "c1", gset01[:, d_, :], gsets[d_, :, 0:1024], (), (b_consts,))
            dma("sp", "c0", gvec01[:, d_, :], gsets[d_, :, 1024:1064], (), (b_consts,))
        b_c2 = B("consts2")
        sc.op("act", lambda e: e.activation(out=esink[:], in_=esink[:], func=AF.Exp), (b_consts,), (b_c2,))
        sc.op("dve", lambda e: e.memset(ones_f[:], 1.0), (), (b_c2,))
        sc.op("dve", lambda e: e.memset(carries[:], 0.0), (), (b_carr,))

        def clam_compute(dst, lamsrc, rd, wr):
            sc.op("act", lambda e: e.activation(out=dst, in_=lamsrc, func=AF.Exp, scale=-1.0), rd, wr)
            sc.op("act", lambda e: e.activation(out=dst, in_=dst, func=AF.Ln, bias=1.0), wr, wr)
            sc.op("act", lambda e: e.mul(out=dst, in_=dst, mul=-8.0), wr, wr)

        for d_ in range(2):
            clam_compute(clam01[:, d_, :], gvec01[:, d_, 8:12], (b_consts,), (b_c2,))

        def load_w(dst, src, K, N, gain, bw):
            last = None
            for dk in range(K):
                for n0 in range(0, N, 2048):
                    n1 = min(N, n0 + 2048)
                    dma("sp", "stg", stage[:, 0:n1 - n0], src[dk * 128:(dk + 1) * 128, n0:n1], (), (b_stage,))
                    if gain is None:
                        last = sc.op("dve", lambda e, dk=dk, n0=n0, n1=n1: e.tensor_copy(out=dst[:, dk, n0:n1], in_=stage[:, 0:n1 - n0]), (b_stage, b_consts), (bw,))
                    else:
                        last = sc.op("dve", lambda e, dk=dk, n0=n0, n1=n1, gain=gain: e.tensor_scalar(out=dst[:, dk, n0:n1], in0=stage[:, 0:n1 - n0], scalar1=gain(dk), scalar2=None, op0=ALU.mult), (b_stage, b_consts), (bw,))
            return last

        load_w(woutb, w_out, 8, D, lambda dk: ogcol[:, dk:dk + 1], b_wout)
        stage_guard = []
        b_uvb = B("uvb")
        uvb_ops = []
        for i in range(64):
            r0, r1 = i * 256, (i + 1) * 256
            uvb_ops.append(dma("pool", "uvc", uvb[r0:r1, 0:D], peer_u[r0:r1, :], (), ()))
            uvb_ops.append(dma("pool", "uvc", uvb[r0:r1, D:2 * D], peer_v[r0:r1, :], (), ()))

        def rms_stats(junk_b, x_ap, n, junk_ap, ss_ap, rstd_ap, rd, wr_small):
            sc.op("act", lambda e: e.activation(out=junk_ap, in_=x_ap, func=AF.Square, accum_out=ss_ap), rd, (wr_small, junk_b))
            sc.op("act", lambda e: e.activation(out=ss_ap, in_=ss_ap, func=AF.Sqrt, scale=1.0 / n, bias=EPS), (wr_small,), (wr_small,))
            sc.op("dve", lambda e: e.reciprocal(out=rstd_ap, in_=ss_ap), (wr_small,), (wr_small,))

        def transposes(src_b, nchunk, rd, evac_out, evac_wr, evac_eng="act"):
            for k in range(nchunk):
                sc.op("pe", lambda e, k=k: e.transpose(pst[:, k * 128:(k + 1) * 128], src_b[:, k * 128:(k + 1) * 128], ident[:]),
                      tuple(rd) + (b_consts,), (b_pst,))
            src = pst[:, 0:nchunk * 128].rearrange("p (a b) -> p a b", b=128)
            if evac_eng == "act":
                sc.op("act", lambda e: e.copy(out=evac_out, in_=src), (b_pst,), evac_wr)
            else:
                sc.op("dve", lambda e: e.tensor_copy(out=evac_out, in_=src), (b_pst,), evac_wr)

        b_xnT = sc.bufs_n("xnT", NT)

        def phase_A1(xsrc, xin_, xnb_):
            b_xin = sc.bufs_n("xin", 2); b_xnb = sc.bufs_n("xnb", 2); b_sm = sc.bufs_n("a1s", 2)
            for j in range(NT):
                s_ = j % 2
                dma("sp", f"xin{s_}", xin_[s_], xsrc[j * 128:(j + 1) * 128, :], (), (b_xin[s_],))
                ss = small[:, 2 * s_:2 * s_ + 1]; rs = small[:, 2 * s_ + 1:2 * s_ + 2]
                rms_stats(b_xnb[s_], xin_[s_], D, xnb_[s_], ss, rs, (b_xin[s_],), b_sm[s_])
                sc.op("dve", lambda e, s_=s_, rs=rs: e.tensor_scalar(out=xnb_[s_], in0=xin_[s_], scalar1=rs, scalar2=None, op0=ALU.mult),
                      (b_xin[s_], b_sm[s_]), (b_xnb[s_],))
                transposes(xnb_[s_], 8, (b_xnb[s_],), xnT[:, :, j * 128:(j + 1) * 128], (b_xnT[j],))

        def proj(wb, col0, c0, c1, bk, rd_extra=()):
            tiles = [b_xnT[j] for j in range(c0 // 128, (c1 + 127) // 128)]
            for dk in range(8):
                sc.op("pe", lambda e, dk=dk: e.matmul(bank[bk][:, 0:c1 - c0], wb[:, dk, col0:col0 + 128], xnT[:, dk, c0:c1], start=(dk == 0), stop=(dk == 7)),
                      tuple(tiles) + tuple(rd_extra), (b_bank[bk],))

        def pieces(c0, c1):
            out = []
            c = c0
            while c < c1:
                out.append((c, min(c1, c + 512)))
                c += 512
            return out

        b_wl = B("wlru"); b_gs = B("gset"); b_xr = B("xr"); b_xc = B("xc"); b_xcb = B("xcb"); b_ra = B("ra"); b_iu = B("iu"); b_sh = B("sh")
        b_acc = B("acc4"); b_gg = B("gg")

        def lru_chunk(c, wsets, full, carry_in, carry_out, rev):
            for (a, b_) in pieces(0, SE):
                proj(wlrub, c * 128, a, b_, 0, (b_wl,))
                sc.op("act", lambda e, a=a, b_=b_: e.copy(out=xr_c[:, a:b_], in_=bank[0][:, 0:b_ - a]), (b_bank[0],), (b_xr,))
            if full:
                for (a, b_) in pieces(128, 128 + S):
                    proj(wlrub, 512 + c * 128, a, b_, 1, (b_wl,))
                    sc.op("act", lambda e, a=a, b_=b_: e.activation(out=gg[:, a - 128:b_ - 128], in_=bank[1][:, 0:b_ - a], func=AF.Gelu_apprx_tanh), (b_bank[1],), (b_gg,))
            gv = wsets[0][1]
            sc.op("dve", lambda e: e.tensor_scalar(out=xc_c[:, :], in0=xr_c[:, 126:126 + S], scalar1=gv[:, 16 + c * 5:17 + c * 5], scalar2=gv[:, 12 + c:13 + c], op0=ALU.mult, op1=ALU.add),
                  (b_xr, b_gs), (b_xc,))
            for j in range(1, 5):
                sc.op("dve", lambda e, j=j: e.scalar_tensor_tensor(out=xc_c[:, :], in0=xr_c[:, 126 + j:126 + j + S], scalar=gv[:, 16 + c * 5 + j:17 + c * 5 + j], in1=xc_c[:, :], op0=ALU.mult, op1=ALU.add),
                      (b_xr, b_gs, b_xc), (b_xc,))
            sc.op("act", lambda e: e.copy(out=xcb_c[:, :], in_=xc_c[:, :]), (b_xc,), (b_xcb,))
            for di, (gs_ap, gv_ap, cl_ap, rv) in enumerate(wsets):
                for (a, b_) in pieces(0, S):
                    sc.op("pe", lambda e, a=a, b_=b_, gs_ap=gs_ap: e.matmul(bank[2][:, 0:b_ - a], gs_ap[:, c * 128:(c + 1) * 128], xcb_c[:, a:b_], start=True, stop=True), (b_xcb, b_gs), (b_bank[2],))
                    sc.op("pe", lambda e, a=a, b_=b_, gs_ap=gs_ap: e.matmul(bank[3][:, 0:b_ - a], gs_ap[:, 512 + c * 128:512 + (c + 1) * 128], xcb_c[:, a:b_], start=True, stop=True), (b_xcb, b_gs), (b_bank[3],))
                    sc.op("act", lambda e, a=a, b_=b_, gv_ap=gv_ap: e.activation(out=ra[:, a:b_], in_=bank[2][:, 0:b_ - a], func=AF.Sigmoid, bias=gv_ap[:, c:c + 1]), (b_bank[2], b_gs), (b_ra,))
                    sc.op("act", lambda e, a=a, b_=b_, gv_ap=gv_ap: e.activation(out=iu[:, a:b_], in_=bank[3][:, 0:b_ - a], func=AF.Sigmoid, bias=gv_ap[:, 4 + c:5 + c]), (b_bank[3], b_gs), (b_iu,))
                sc.op("act", lambda e, cl_ap=cl_ap: e.activation(out=ra[:, :], in_=ra[:, :], func=AF.Exp, scale=cl_ap[:, c:c + 1]), (b_ra, b_gs), (b_ra,))
                sc.op("act", lambda e: e.activation(out=sh[:, :], in_=ra[:, :], func=AF.Square), (b_ra,), (b_sh,))
                sc.op("act", lambda e: e.activation(out=sh[:, :], in_=sh[:, :], func=AF.Sqrt, scale=-1.0, bias=1.0), (b_sh,), (b_sh,))
                sc.op("dve", lambda e: e.tensor_tensor(out=iu[:, :], in0=iu[:, :], in1=xc_c[:, :], op=ALU.mult), (b_iu, b_xc), (b_iu,))
                sc.op("dve", lambda e: e.tensor_tensor(out=iu[:, :], in0=iu[:, :], in1=sh[:, :], op=ALU.mult), (b_iu, b_sh), (b_iu,))
                ci = carry_in[di]
                if rv:
                    sc.op("dve", lambda e, ci=ci: e.tensor_tensor_scan(out=apx(sh, [[-1, S]], S - 1), data0=apx(ra, [[-1, S]], S - 1), data1=apx(iu, [[-1, S]], S - 1), initial=carries[:, ci, c:c + 1], op0=ALU.mult, op1=ALU.add),
                          (b_ra, b_iu, b_carr), (b_sh,))
                    last = sh[:, 0:1]
                else:
                    sc.op("dve", lambda e, ci=ci: e.tensor_tensor_scan(out=sh[:, :], data0=ra[:, :], data1=iu[:, :], initial=carries[:, ci, c:c + 1], op0=ALU.mult, op1=ALU.add),
                          (b_ra, b_iu, b_carr), (b_sh,))
                    last = sh[:, S - 1:S]
                co = carry_out[di]
                if co is not None:
                    sc.op("dve", lambda e, co=co, last=last: e.tensor_copy(out=carries[:, co, c:c + 1], in_=last), (b_sh, b_carr), (b_carr,))
                if full:
                    if di == 0:
                        sc.op("dve", lambda e: e.tensor_tensor(out=acc4[:, :], in0=gg[:, :], in1=sh[:, :], op=ALU.mult), (b_gg, b_sh), (b_acc,))
                    else:
                        sc.op("dve", lambda e: e.tensor_tensor(out=gg[:, :], in0=gg[:, :], in1=sh[:, :], op=ALU.mult), (b_gg, b_sh), (b_gg,))
                        sc.op("dve", lambda e: e.tensor_tensor(out=acc4[:, :], in0=acc4[:, :], in1=gg[:, :], op=ALU.add), (b_gg, b_acc), (b_acc,))
            if full:
                sc.op("act", lambda e: e.copy(out=lruT[:, c, :], in_=acc4[:, :]), (b_acc,), (b_lruT[c],))
                sc.op("act", lambda e: e.activation(out=acc4[:, :], in_=acc4[:, :], func=AF.Square), (b_acc,), (b_acc,))
                for j in range(T):
                    sc.op("pe", lambda e, j=j: e.matmul(bank[4][:, j:j + 1], acc4[:, j * 128:(j + 1) * 128], ones_f[:, 0:1], start=True, stop=True), (b_acc, b_c2), (b_bank[4],))
                if c == 0:
                    sc.op("dve", lambda e: e.tensor_copy(out=ssl[:, :], in_=bank[4][:, 0:T]), (b_bank[4],), (b_ssl,))
                else:
                    sc.op("dve", lambda e: e.tensor_tensor(out=ssl[:, :], in0=ssl[:, :], in1=bank[4][:, 0:T], op=ALU.add), (b_bank[4], b_ssl), (b_ssl,))

        def load_wlru():
            load_w(wlrub, w_lru, 8, NLRU, lambda dk: gcol[:, 0, dk:dk + 1], b_wl)

        def light_pass(slot, carry_in_idx, carry_out_idx):
            sc.barrier()
            load_wlru()
            dma("pool", "gsl", gsetl[:, :], gsets[2 + slot, :, 0:1024], (), (b_gs,))
            dma("sp", "gvl", gvecl[:, :], gsets[2 + slot, :, 1024:1064], (), (b_gs,))
            clam_compute(claml[:, :], gvecl[:, 8:12], (b_gs,), (b_gs,))
            phase_A1(x_light[slot], xinL, xnbL)
            for c in range(4):
                lru_chunk(c, [(gsetl, gvecl, claml, False)], False, [carry_in_idx], [carry_out_idx], False)

        for slot in range(6):
            if slot == 0:
                cin = 9
            else:
                sc.op("dve", lambda e, slot=slot: e.tensor_scalar(out=carries[:, 11, :], in0=carries[:, slot - 1, :], scalar1=cfg[:, slot:slot + 1], scalar2=None, op0=ALU.mult), (b_carr, b_consts), (b_carr,))
                cin = 11
            light_pass(slot, cin, slot)
        for (dst, off) in ((7, 8), (8, 16)):
            sc.op("dve", lambda e, dst=dst, off=off: e.tensor_scalar(out=carries[:, dst, :], in0=carries[:, 0, :], scalar1=cfg[:, off:off + 1], scalar2=None, op0=ALU.mult), (b_carr, b_consts), (b_carr,))
            for s_ in range(1, 6):
                sc.op("dve", lambda e, dst=dst, off=off, s_=s_: e.scalar_tensor_tensor(out=carries[:, dst, :], in0=carries[:, s_, :], scalar=cfg[:, off + s_:off + s_ + 1], in1=carries[:, dst, :], op0=ALU.mult, op1=ALU.add), (b_carr, b_consts), (b_carr,))
        light_pass(6, 8, 6)

        b_wq = B("wqkv"); b_rope = B("rope"); b_qr = sc.bufs_n("qr", T); b_kr = sc.bufs_n("kr", NT); b_v = sc.bufs_n("v", NT)
        b_E = sc.bufs_n("E", 3); b_rt = sc.bufs_n("ropet", 2); b_af = B("attnf"); b_ab = B("attnb"); b_bm = B("bmask")

        def full_pass(slot, cf_in, cb_in, cf_out):
            sc.barrier()
            load_w(wqkvb, w_qkv, 8, NQKV, lambda dk: gcol[:, 0, dk:dk + 1], b_wq)
            dma("sp", "rope", cosT[:, :], rope[slot, 0, :, :], (), (b_rope,))
            dma("sp", "rope", sinT[:, :], rope[slot, 1, :, :], (), (b_rope,))
            dma("sp", "bm", bmask[:, :, :], masks[2 + 2 * slot:4 + 2 * slot].rearrange("m p q -> p m q"), (), (b_bm,))
            phase_A1(x_full[slot], xin, xnb)

            def rope_piece(colq, colsw, a, b_, dst, wr):
                proj(wqkvb, colq, a, b_, 0, (b_wq,))
                proj(wqkvb, colsw, a, b_, 1, (b_wq,))
                n = b_ - a
                sc.op("dve", lambda e: e.tensor_tensor(out=ropet[0][:, 0:n], in0=bank[0][:, 0:n], in1=cosT[:, a:b_], op=ALU.mult), (b_bank[0], b_rope), (b_rt[0],))
                sc.op("dve", lambda e: e.tensor_tensor(out=ropet[1][:, 0:n], in0=bank[1][:, 0:n], in1=sinT[:, a:b_], op=ALU.mult), (b_bank[1], b_rope), (b_rt[1],))
                sc.op("dve", lambda e: e.tensor_tensor(out=dst, in0=ropet[0][:, 0:n], in1=ropet[1][:, 0:n], op=ALU.add), (b_rt[0], b_rt[1]), wr)

            for m in range(4):
                for (a, b_) in pieces(128, 128 + S):
                    rope_piece(m * 128, 512 + m * 128, a, b_, qrT[:, m, a - 128:b_ - 128], tuple(b_qr[j] for j in range((a - 128) // 128, (b_ - 128 + 127) // 128)))
            for (a, b_) in pieces(0, SE):
                rope_piece(1024, 1152, a, b_, krT[:, a:b_], tuple(b_kr[j] for j in range(a // 128, (b_ + 127) // 128)))
            for j in range(NT):
                for dk in range(8):
                    sc.op("pe", lambda e, dk=dk, j=j: e.matmul(bank[2][:, 0:128], xnT[:, dk, j * 128:(j + 1) * 128], wqkvb[:, dk, 1280:1408], start=(dk == 0), stop=(dk == 7)), (b_xnT[j], b_wq), (b_bank[2],))
                sc.op("act", lambda e, j=j: e.copy(out=apx(vaug, [[65, 2], [1, 64]], j * 130), in_=bank[2][:, 0:128].rearrange("p (g d) -> p g d", d=64)), (b_bank[2],), (b_v[j],))
                sc.op("dve", lambda e, j=j: e.memset(apx(vaug, [[65, 2], [1, 1]], j * 130 + 64), 1.0), (), (b_v[j],))
            for n in range(T):
                j = n + 1
                for g in range(2):
                    cbs = (j - 1, j, j + 1)
                    for ci, cb in enumerate(cbs):
                        sc.op("pe", lambda e, cb=cb, g=g, n=n, ci=ci: e.matmul(bank[ci][:, :], krT[64 * g:64 * g + 64, cb * 128:(cb + 1) * 128],
                                                                                apx(qrT[64 * g:64 * g + 64, 0, :], [[S, 4], [1, 128]], n * 128), start=True, stop=True),
                              (b_kr[cb], b_qr[n]), (b_bank[ci],))
                        sc.op("act", lambda e, ci=ci: e.activation(out=Et[ci][:, :], in_=bank[ci][:, :], func=AF.Exp, scale=0.125), (b_bank[ci],), (b_E[ci],))
                    for ci, mi in ((0, 0), (2, 1)):
                        if (ci == 0 and n == 0) or (ci == 2 and n == T - 1):
                            msrc = bmask[:, mi, :]; rdm = b_bm
                        else:
                            msrc = mstd[:, mi, :]; rdm = b_consts
                        sc.op("dve", lambda e, ci=ci, msrc=msrc: e.tensor_tensor(out=Et[ci][:, :].rearrange("p (h q) -> p h q", q=128), in0=Et[ci][:, :].rearrange("p (h q) -> p h q", q=128),
                                                                               in1=apx(msrc, [[0, 4], [1, 128]]), op=ALU.mult), (b_E[ci], rdm), (b_E[ci],))
                    ob = 3 + g
                    for hh in range(4):
                        for ci, cb in enumerate(cbs):
                            sc.op("pe", lambda e, hh=hh, ci=ci, cb=cb, g=g, ob=ob: e.matmul(bank[ob][:, hh * 65:(hh + 1) * 65], Et[ci][:, hh * 128:(hh + 1) * 128], vaug[:, cb, g * 65:(g + 1) * 65], start=(ci == 0), stop=(ci == 2)),
                                  (b_E[ci], b_v[cb]), (b_bank[ob],))
                    den = small[:, 8 + 4 * g:12 + 4 * g]
                    sc.op("dve", lambda e, ob=ob, g=g, den=den: e.tensor_tensor(out=den, in0=apx(bank[ob][:, 0:1], [[65, 4]], 64), in1=esink[:, 4 * g:4 * g + 4], op=ALU.add), (b_bank[ob], b_c2), (b_small,))
                    sc.op("dve", lambda e, den=den: e.reciprocal(out=den, in_=den), (b_small,), (b_small,))
                    sc.op("dve", lambda e, ob=ob, g=g, den=den: e.tensor_tensor(out=attnf[:, g * 256:(g + 1) * 256].rearrange("p (h d) -> p h d", d=64), in0=apx(bank[ob][:, 0:1], [[65, 4], [1, 64]]),
                                                                                in1=apx(den, [[1, 4], [0, 64]]), op=ALU.mult), (b_bank[ob], b_small), (b_af,))
                ss = small[:, 16:17]; rs = small[:, 17:18]
                rms_stats(b_ab, attnf[:, :], 512, attnb[:, :], ss, rs, (b_af,), b_small)
                sc.op("dve", lambda e, rs=rs: e.tensor_scalar(out=attnb[:, :], in0=attnf[:, :], scalar1=rs, scalar2=None, op0=ALU.mult), (b_af, b_ab, b_small), (b_ab,))
                transposes(attnb, 4, (b_ab,), attnT[:, :, n * 128:(n + 1) * 128], (b_attnT[n],))
            sc.barrier()
            load_wlru()
            for c in range(4):
                lru_chunk(c, [(gset01[:, 0, :], gvec01[:, 0, :], clam01[:, 0, :], False), (gset01[:, 1, :], gvec01[:, 1, :], clam01[:, 1, :], True)], True,
                          [cf_in, cb_in], [cf_out, None], True)
            sc.op("act", lambda e: e.activation(out=rstdl[:, :], in_=ssl[:, :], func=AF.Sqrt, scale=1.0 / 512, bias=EPS), (b_ssl,), (b_rstdl,))
            sc.op("dve", lambda e: e.reciprocal(out=rstdl[:, :], in_=rstdl[:, :]), (b_rstdl,), (b_rstdl,))
            sc.barrier()
            phase_B(slot)

        b_wpq = B("wpq"); b_keys = B("keys"); b_wg = B("wg"); b_wp = B("wp")
        b_xB = B("xB"); b_h1s = sc.bufs_n("h1", 2); b_xn2 = B("xn2"); b_xn2b = B("xn2b"); b_xn2T = B("xn2T"); b_qpT = B("qpT"); b_subs = B("subs"); b_subt = B("subt")
        b_uv = sc.bufs_n("uvbuf", 8); b_dg = sc.bufs_n("diag", 4); b_dk = sc.bufs_n("dk", 8); b_wk = sc.bufs_n("wk", 8); b_xgs = sc.bufs_n("xn2gb", 2); b_junk = B("junk"); b_junkL = B("junkL")
        b_pBs = sc.bufs_n("pB", 2); b_pBb = B("pBb"); b_pT = B("pT"); b_gate = B("gate")
        b_tk = B("topk"); b_bsm = B("bsmall"); b_eids = sc.bufs_n("eid", 2); b_gsms = sc.bufs_n("gsm", 2)

        def pre_gen(slot, n):
            par = n % 2
            j = n + 1
            h1 = h1s[par]; b_h1 = b_h1s[par]; pB = pBs[par]; b_pB = b_pBs[par]
            xn2gb = xn2gbs[par]; b_xg = b_xgs[par]; eidi = eidis[par]; b_eid = b_eids[par]; gsm = gsms[par]; b_gs_ = b_gsms[par]
            dma("sp", "xB", xB[:, :], x_full[slot, j * 128:(j + 1) * 128, :], (), (b_xB,))
            dma("sp", f"pB{par}", pB[:, :], p_in[slot, n * 128:(n + 1) * 128, :], (), (b_pB,))
            yield
            for half in range(2):
                for m in range(4):
                    sc.op("pe", lambda e, m=m, half=half: e.matmul(bank[half][:, :], attnT[:, m, n * 128:(n + 1) * 128], woutb[:, m, half * 512:(half + 1) * 512], start=(m == 0), stop=(m == 3)),
                          (b_attnT[n], b_wout), (b_bank[half],))
                for c in range(4):
                    sc.op("pe", lambda e, c=c, half=half: e.matmul(bank[2 + half][:, :], lruT[:, c, n * 128:(n + 1) * 128], woutb[:, 4 + c, half * 512:(half + 1) * 512], start=(c == 0), stop=(c == 3)),
                          (b_lruT[c], b_wout), (b_bank[2 + half],))
                hs = slice(half * 512, (half + 1) * 512)
                sc.op("dve", lambda e, half=half, hs=hs: e.tensor_tensor(out=h1[:, hs], in0=bank[half][:, :], in1=xB[:, hs], op=ALU.add), (b_bank[half], b_xB), (b_h1,))
                sc.op("dve", lambda e, half=half, hs=hs: e.scalar_tensor_tensor(out=h1[:, hs], in0=bank[2 + half][:, :], scalar=rstdl[:, n:n + 1], in1=h1[:, hs], op0=ALU.mult, op1=ALU.add),
                      (b_bank[2 + half], b_rstdl, b_h1), (b_h1,))
                yield
            ss = small[:, 20:21]; rs = small[:, 21:22]
            rms_stats(b_junk, h1[:, :], D, junk[:, :], ss, rs, (b_h1,), b_bsm)
            yield
            sc.op("dve", lambda e: e.tensor_scalar(out=xn2b[:, :], in0=h1[:, :], scalar1=rs, scalar2=None, op0=ALU.mult), (b_h1, b_bsm), (b_xn2b,))
            yield
            sc.op("dve", lambda e: e.scalar_tensor_tensor(out=xn2gb[:, :], in0=h1[:, :], scalar=rs, in1=grow[:, 0, :], op0=ALU.mult, op1=ALU.mult), (b_h1, b_bsm, b_consts), (b_xg,))
            yield
            transposes(xn2b, 8, (b_xn2b,), xn2T[:, :, :], (b_xn2T,))
            yield
            for grp in range(4):
                for cc in range(4):
                    ch = grp * 4 + cc
                    for dk in range(8):
                        sc.op("pe", lambda e, dk=dk, ch=ch, cc=cc, grp=grp: e.matmul(bank[grp][:, cc * 128:(cc + 1) * 128], wpqb[:, dk, ch * 128:(ch + 1) * 128], xn2T[:, dk, :], start=(dk == 0), stop=(dk == 7)),
                              (b_xn2T, b_wpq), (b_bank[grp],))
                sc.op("act", lambda e, grp=grp: e.copy(out=qpT[:, grp * 4:(grp + 1) * 4, :], in_=bank[grp][:, :].rearrange("p (a b) -> p a b", b=128)), (b_bank[grp],), (b_qpT,))
                yield
            for grp in range(4):
                for cc in range(4):
                    ch = grp * 4 + cc
                    sc.op("pe", lambda e, ch=ch, cc=cc, grp=grp: e.matmul(bank[grp][:, cc * 128:(cc + 1) * 128], qpT[:, ch, :], keysb[:, ch, :], start=True, stop=True), (b_qpT, b_keys), (b_bank[grp],))
                sc.op("act", lambda e, grp=grp: e.copy(out=subs[:, grp * 4:(grp + 1) * 4, :], in_=bank[grp][:, :].rearrange("p (a b) -> p a b", b=128)), (b_bank[grp],), (b_subs,))
                yield
            for ch in range(16):
                sc.op("dve", lambda e, ch=ch: e.max(out=top0[:, ch, 0:8], in_=subs[:, ch, :]), (b_subs,), (b_tk,))
                sc.op("dve", lambda e, ch=ch: e.max_index(out=idxu[:, ch, 0:8], in_max=top0[:, ch, 0:8], in_values=subs[:, ch, :]), (b_subs, b_tk), (b_tk,))
                sc.op("dve", lambda e, ch=ch: e.match_replace(out=subt[:, :], in_to_replace=top0[:, ch, 0:8], in_values=subs[:, ch, :], imm_value=-1e30), (b_subs, b_tk), (b_subt,))
                sc.op("dve", lambda e, ch=ch: e.max(out=top0[:, ch, 8:16], in_=subt[:, :]), (b_subt,), (b_tk,))
                sc.op("dve", lambda e, ch=ch: e.max_index(out=idxu[:, ch, 8:16], in_max=top0[:, ch, 8:16], in_values=subt[:, :]), (b_subt, b_tk), (b_tk,))
                yield
            sc.op("dve", lambda e: e.tensor_copy(out=idxf[:, :, :], in_=idxu[:, :, :]), (b_tk,), (b_tk,))
            sc.op("dve", lambda e: e.tensor_tensor(out=cand[:, :, :].rearrange("p h (a b) -> p h a b", b=16), in0=apx(top0[:, 0, :], [[32, 8], [1, 16], [0, 16]]), in1=apx(top0[:, 0, :], [[32, 8], [0, 16], [1, 16]], 16), op=ALU.add),
                  (b_tk,), (b_tk, b_subs))
            yield
            for h in range(8):
                sc.op("dve", lambda e, h=h: e.max(out=cbest[:, h, 0:8], in_=cand[:, h, :]), (b_tk,), (b_tk,))
                sc.op("dve", lambda e, h=h: e.max_index(out=posu[:, h, 0:8], in_max=cbest[:, h, 0:8], in_values=cand[:, h, :]), (b_tk,), (b_tk,))
                sc.op("dve", lambda e, h=h: e.match_replace(out=candt[:, :], in_to_replace=cbest[:, h, 0:8], in_values=cand[:, h, :], imm_value=-1e30), (b_tk,), (b_subt,))
                sc.op("dve", lambda e, h=h: e.max(out=cbest[:, h, 8:16], in_=candt[:, :]), (b_subt,), (b_tk,))
                sc.op("dve", lambda e, h=h: e.max_index(out=posu[:, h, 8:16], in_max=cbest[:, h, 8:16], in_values=candt[:, :]), (b_subt, b_tk), (b_tk,))
                yield
            sc.op("dve", lambda e: e.tensor_single_scalar(out=posf[:, :, :].bitcast(U32), in_=posu[:, :, :], scalar=4, op=ALU.logical_shift_right), (b_tk,), (b_tk,))
            sc.op("dve", lambda e: e.tensor_copy(out=afl[:, :, :], in_=posf[:, :, :].bitcast(U32)), (b_tk,), (b_tk,))
            sc.op("dve", lambda e: e.tensor_single_scalar(out=posf[:, :, :].bitcast(U32), in_=posu[:, :, :], scalar=15, op=ALU.bitwise_and), (b_tk,), (b_tk,))
            sc.op("dve", lambda e: e.tensor_copy(out=bfl[:, :, :], in_=posf[:, :, :].bitcast(U32)), (b_tk,), (b_tk,))
            yield
            for (sel, which, dst) in ((afl, 0, isel), (bfl, 1, jsel)):
                sc.op("dve", lambda e, sel=sel: e.tensor_tensor(out=oh[:, :, :], in0=apx(sel[:, 0, :], [[1, 128], [0, 16]]), in1=apx(iota16[:, :], [[0, 128], [1, 16]]), op=ALU.is_equal), (b_tk, b_consts), (b_tk, b_subs))
                yield
                sc.op("dve", lambda e, which=which: e.tensor_tensor(out=oh[:, :, :].rearrange("p (h k) a -> p h k a", k=16), in0=oh[:, :, :].rearrange("p (h k) a -> p h k a", k=16),
                                                                      in1=apx(idxf[:, 0, :], [[32, 8], [0, 16], [1, 16]], 16 * which), op=ALU.mult), (b_tk,), (b_tk,))
                yield
                sc.op("dve", lambda e, dst=dst: e.tensor_reduce(out=dst[:, :], in_=oh[:, :, :], axis=AX.X, op=ALU.add), (b_tk,), (b_tk,))
                yield
            sc.op("dve", lambda e: e.scalar_tensor_tensor(out=eidf[:, :], in0=isel[:, :], scalar=128.0, in1=jsel[:, :], op0=ALU.mult, op1=ALU.add), (b_tk,), (b_tk,))
            sc.op("dve", lambda e: e.tensor_copy(out=eidi[:, :], in_=eidf[:, :]), (b_tk,), (b_eid,))
            yield
            sc.op("dve", lambda e: e.tensor_tensor(out=gsm[:, :, :], in0=cbest[:, :, :], in1=apx(cbest[:, 0, :], [[16, 8], [0, 16]]), op=ALU.subtract), (b_tk,), (b_gs_,))
            sc.op("act", lambda e: e.activation(out=gsm[:, :, :], in_=gsm[:, :, :], func=AF.Exp), (b_gs_,), (b_gs_,))
            sc.op("dve", lambda e: e.tensor_reduce(out=small[:, 24:32], in_=gsm[:, :, :], axis=AX.X, op=ALU.add), (b_gs_,), (b_bsm,))
            sc.op("dve", lambda e: e.reciprocal(out=small[:, 24:32], in_=small[:, 24:32]), (b_bsm,), (b_bsm,))
            sc.op("dve", lambda e: e.tensor_tensor(out=gsm[:, :, :], in0=gsm[:, :, :], in1=apx(small[:, 24:32], [[1, 8], [0, 16]]), op=ALU.mult), (b_gs_, b_bsm), (b_gs_,))
            yield

        def loop_front(n, kk):
            par = n % 2
            xn2gb = xn2gbs[par]; b_xg = b_xgs[par]; eidi = eidis[par]; b_eid = b_eids[par]
            s_ = kk % NSLOT; s8 = kk % 8
            o = sc.op("pool", lambda e: e.indirect_dma_start(out=uvbuf[s_][:, :], out_offset=None, in_=uvb[:, :], in_offset=bass.IndirectOffsetOnAxis(ap=eidi[:, kk:kk + 1], axis=0)),
                      (b_eid,), (b_uv[s_],), dma=f"ug{s_}")
            if s_ >= 6:
                o.deps.extend(stage_guard)
            if uvb_ops:
                o.deps.extend(uvb_ops)
                uvb_ops.clear()
            sc.op("dve", lambda e: e.scalar_tensor_tensor(out=junkL[:, :], in0=uvbuf[s_][:, 0:D], scalar=1.0, in1=xn2gb[:, :], op0=ALU.mult, op1=ALU.mult, accum_out=dots[:, kk:kk + 1]),
                  (b_uv[s_], b_xg), (b_junkL, b_dk[s8]))
            gsm = gsms[par]; b_gs_ = b_gsms[par]
            gflat = gsm[:, :, :].rearrange("p h k -> p (h k)")
            s2 = kk % 4
            sc.op("act", lambda e: e.activation(out=wts[:, kk:kk + 1], in_=dots[:, kk:kk + 1], func=AF.Gelu_apprx_tanh), (b_dk[s8],), (b_wk[s8],))
            sc.op("act", lambda e: e.mul(out=wts[:, kk:kk + 1], in_=wts[:, kk:kk + 1], mul=gflat[:, kk:kk + 1]), (b_wk[s8], b_gs_), (b_wk[s8],))
            sc.op("act", lambda e: e.mul(out=diag[s2][:, :], in_=ident[:, :], mul=wts[:, kk:kk + 1]), (b_wk[s8], b_consts), (b_dg[s2],))

        def loop_back(n, kk):
            s_ = kk % NSLOT; s2 = kk % 4; s8 = kk % 8
            for half in range(2):
                sc.op("pe", lambda e, half=half: e.matmul(bank[5 + half][:, :], diag[s2][:, :], uvbuf[s_][:, D + half * 512:D + (half + 1) * 512], start=(kk == 0), stop=(kk == 127)),
                      (b_dg[s2], b_uv[s_]), (b_bank[5 + half],))

        def post_gen(slot, n):
            par = n % 2
            h1 = h1s[par]; b_h1 = b_h1s[par]; pB = pBs[par]; b_pB = b_pBs[par]
            ss = small[:, 22:23]; rs = small[:, 23:24]
            rms_stats(b_junk, h1[:, :], D, junk[:, :], ss, rs, (b_h1,), b_bsm)
            yield
            sc.op("dve", lambda e: e.tensor_scalar(out=xn2b[:, :], in0=h1[:, :], scalar1=rs, scalar2=None, op0=ALU.mult), (b_h1, b_bsm), (b_xn2b,))
            yield
            transposes(xn2b, 8, (b_xn2b,), xn2T[:, :, :], (b_xn2T,))
            yield
            sc.op("dve", lambda e: e.tensor_copy(out=pBb[:, :], in_=pB[:, :]), (b_pB,), (b_pBb,))
            transposes(pBb, 2, (b_pBb,), pT[:, :, :], (b_pT,))
            yield
            for half in range(2):
                for dk in range(8):
                    sc.op("pe", lambda e, dk=dk, half=half: e.matmul(bank[half][:, :], xn2T[:, dk, :], wgb[:, dk, half * 512:(half + 1) * 512], start=(dk == 0), stop=(dk == 7)), (b_xn2T, b_wg), (b_bank[half],))
                for kc in range(2):
                    sc.op("pe", lambda e, kc=kc, half=half: e.matmul(bank[2 + half][:, :], pT[:, kc, :], wpb[:, kc, half * 512:(half + 1) * 512], start=(kc == 0), stop=(kc == 1)), (b_pT, b_wp), (b_bank[2 + half],))
                hs = slice(half * 512, (half + 1) * 512)
                sc.op("act", lambda e, half=half, hs=hs: e.activation(out=gate[:, hs], in_=bank[half][:, :], func=AF.Sigmoid), (b_bank[half],), (b_gate,))
                yield
                sc.op("dve", lambda e, half=half, hs=hs: e.tensor_tensor(out=gate[:, hs], in0=gate[:, hs], in1=bank[2 + half][:, :], op=ALU.mult), (b_gate, b_bank[2 + half]), (b_gate,))
                yield
            sc.op("dve", lambda e: e.tensor_tensor(out=h1[:, :], in0=h1[:, :], in1=gate[:, :], op=ALU.add), (b_h1, b_gate), (b_h1,))
            yield
            ss2 = small[:, 32:33]; rs2 = small[:, 33:34]
            rms_stats(b_junk, h1[:, :], D, junk[:, :], ss2, rs2, (b_h1,), b_bsm)
            yield
            sc.op("dve", lambda e: e.scalar_tensor_tensor(out=gate[:, :], in0=h1[:, :], scalar=rs2, in1=grow[:, 1, :], op0=ALU.mult, op1=ALU.mult), (b_h1, b_bsm, b_consts), (b_gate,))
            dma("sp", "yout", y_out[slot, n * 128:(n + 1) * 128, :], gate[:, :], (b_gate,), ())
            yield

        def phase_B(slot):
            g1 = load_w(wpqb, w_pq, 8, 2048, lambda dk: gcol[:, 1, dk:dk + 1], b_wpq)
            g2 = load_w(wgb, w_gate, 8, D, lambda dk: gcol[:, 2, dk:dk + 1], b_wg)
            g3 = load_w(wpb, w_proj, 2, D, None, b_wp)
            stage_guard[:] = [g1, g2, g3]
            dma("pool", "keys", keysb[:, :, :], keysT[:, :, :], (), (b_keys,))
            for _ in pre_gen(slot, 0):
                pass
            prev_post = None
            for n in range(T):
                gens = []
                if prev_post is not None:
                    gens.append(prev_post)
                if n + 1 < T:
                    gens.append(pre_gen(slot, n + 1))

                def side():
                    for g in gens:
                        yield from g
                side_it = side()
                LAG = 1
                for kk in range(128 + LAG):
                    if kk < 128:
                        loop_front(n, kk)
                    if kk >= LAG:
                        loop_back(n, kk - LAG)
                    next(side_it, None)
                par = n % 2
                for half in range(2):
                    hs = slice(half * 512, (half + 1) * 512)
                    sc.op("dve", lambda e, half=half, hs=hs, par=par: e.tensor_tensor(out=h1s[par][:, hs], in0=h1s[par][:, hs], in1=bank[5 + half][:, :], op=ALU.add), (b_h1s[par], b_bank[5 + half]), (b_h1s[par],))
                for _ in side_it:
                    pass
                prev_post = post_gen(slot, n)
            for _ in prev_post:
                pass

        for slot in range(4):
            full_pass(slot, 9, 9, None)
        full_pass(4, 7, 6, 10)
        full_pass(5, 10, 8, None)

        sc.emit(nc, stack)
    return nc, sc


def _cols(a):
    return np.ascontiguousarray(a.reshape(-1, 128).T)


def _prep_shared(inp, T):
    f = np.float32
    w_in = inp["w_in"][0]
    qcols = np.zeros(512, np.int64); qsw = np.zeros(512, np.int64)
    for m in range(4):
        for p in range(128):
            head = m + 4 * (p // 64); d = p % 64
            qcols[m * 128 + p] = head * 64 + d
            qsw[m * 128 + p] = head * 64 + (d + 32) % 64
    kcols = 512 + np.arange(128)
    ksw = 512 + (np.arange(128) // 64) * 64 + (np.arange(128) % 64 + 32) % 64
    vcols = 640 + np.arange(128)
    w_qkv = np.ascontiguousarray(w_in[:, np.concatenate([qcols, qsw, kcols, ksw, vcols])])
    w_lru = np.ascontiguousarray(w_in[:, 768:1792])
    g_cols = np.zeros((128, 4, 8), f)
    g_cols[:, 0] = _cols(inp["mix_norm_g"][0]); g_cols[:, 1] = _cols(inp["ffn_norm_g"][0]); g_cols[:, 2] = _cols(inp["ple_norm_g"][0])
    g_rows = np.stack([inp["ffn_norm_g"][0], inp["final_norm_g"]]).astype(f)
    og_cols = np.concatenate([_cols(inp["attn_out_norm_g"][0]), _cols(inp["lru_out_norm_g"][0])], axis=1).astype(f)
    sink = inp["attn_sink"].reshape(1, 8).astype(f)
    keysT = np.ascontiguousarray(inp["peer_keys"][0].reshape(16, 128, 128).transpose(2, 0, 1))
    ident = np.eye(128, dtype=f).astype(ml_dtypes.bfloat16)
    iota16 = np.tile(np.arange(16, dtype=f)[None, :], (128, 1))

    def gset(direction, flip):
        g = np.zeros((128, 8 * 128 + 40), f)
        for wi, wname in enumerate(("lru_wa", "lru_wx")):
            w = inp[wname][0, direction]
            for c in range(4):
                for bb in range(2):
                    blk = 2 * c + bb
                    g[bb * 64:(bb + 1) * 64, wi * 512 + c * 128 + bb * 64: wi * 512 + c * 128 + (bb + 1) * 64] = w[blk]
        g[:, 1024:1028] = _cols(inp["lru_ba"][0, direction]); g[:, 1028:1032] = _cols(inp["lru_bx"][0, direction])
        g[:, 1032:1036] = _cols(inp["lru_lambda"][0, direction]); g[:, 1036:1040] = _cols(inp["conv_b"][0])
        cw = inp["conv_w"][0]
        w5 = np.zeros((5, 512), f)
        if flip:
            w5[1:5] = cw[::-1]
        else:
            w5[0:4] = cw
        for c in range(4):
            g[:, 1040 + c * 5:1040 + (c + 1) * 5] = w5[:, c * 128:(c + 1) * 128].T
        return g
    return dict(w_qkv=w_qkv, w_lru=w_lru, g_cols=g_cols, g_rows=g_rows, og_cols=og_cols, sink=sink, keysT=keysT, ident=ident,
                iota16=iota16, w_out=np.ascontiguousarray(inp["w_out"][0]), w_pq=np.ascontiguousarray(inp["peer_wq"][0]),
                peer_u=inp["peer_u"][0], peer_v=inp["peer_v"][0], w_gate=np.ascontiguousarray(inp["ple_w_gate"][0]),
                w_proj=np.ascontiguousarray(inp["ple_w_proj"][0])), gset


def _seg(xseq, start, S):
    n = xseq.shape[0]
    out = np.zeros((S + 256, xseq.shape[1]), np.float32)
    lo = max(0, start - 128); hi = min(n, start + S + 128)
    out[lo - (start - 128):hi - (start - 128)] = xseq[lo:hi]
    return out


def _rope_tab(start, SE):
    pos = (np.arange(SE, dtype=np.float32) + np.float32(start - 128)).astype(np.float32)
    half = 32
    inv = (np.float32(10000.0) ** (-np.arange(half, dtype=np.float32) / np.float32(half))).astype(np.float32)
    ang = (pos[None, :] * inv[:, None]).astype(np.float32)
    c = np.cos(ang).astype(np.float32); s_ = np.sin(ang).astype(np.float32)
    cosT = np.zeros((128, SE), np.float32); sinT = np.zeros((128, SE), np.float32)
    for p in range(128):
        d = p % 64
        cosT[p] = c[d % 32]
        sinT[p] = -s_[d % 32] if d < 32 else s_[d % 32]
    return np.stack([cosT, sinT])


def kernel(**inp):
    inp = {k: np.asarray(v) for k, v in inp.items()}
    xp = inp["x_prompt"]; xs = inp["x_sample"]; pp = inp["p_prompt"][0]; ps = inp["p_sample"][0]
    S = xp.shape[1]; T = S // 128; SE = S + 256
    assert xp.shape[0] == 32 and xs.shape[0] == 2 and xs.shape[1] == 8 * S
    shared, gset = _prep_shared(inp, T)
    cq = np.arange(128)[:, None]; qq = np.arange(128)[None, :]
    m_prev = (cq >= qq).astype(np.float32); m_next = (cq <= qq).astype(np.float32); m_zero = np.zeros((128, 128), np.float32)
    rope_p = _rope_tab(0, SE)
    in_maps = []
    for c in range(NCORES):
        sq, q = c // 4, c % 4
        xseq = xs[sq]; xrev = xseq[::-1]
        t0 = q * 2 * S
        x_full = np.stack([_seg(xp[4 * c + k], 0, S) for k in range(4)] + [_seg(xseq, t0, S), _seg(xseq, t0 + S, S)])
        p_in = np.stack([pp[4 * c + k] for k in range(4)] + [ps[sq, t0:t0 + S], ps[sq, t0 + S:t0 + 2 * S]])
        nF = 2 * q; nB = 6 - 2 * q
        lights = [_seg(xseq, s_ * S, S) for s_ in range(nF)] + [_seg(xrev, r * S, S) for r in range(nB)] + [_seg(xrev, nB * S, S)]
        x_light = np.stack(lights)
        gs = [gset(0, False), gset(1, False)] + [gset(0, False)] * nF + [gset(1, True)] * nB + [gset(1, True)]
        cfg = np.zeros((128, 32), np.float32)
        for s_ in range(6):
            cfg[:, s_] = 0.0 if (s_ == 0 or s_ == nF) else 1.0
        if nF > 0:
            cfg[:, 8 + nF - 1] = 1.0
        if nB > 0:
            cfg[:, 16 + 5] = 1.0
        rope = np.stack([rope_p] * 4 + [_rope_tab(t0, SE), _rope_tab(t0 + S, SE)])
        mk = [m_prev, m_next]
        for k in range(4):
            mk += [m_zero, m_zero]
        mk += [m_prev if q > 0 else m_zero, m_prev]
        mk += [m_prev, m_next if q < 3 else m_zero]
        mk[2 + 2 * 4 + 1] = m_next
        masks = np.stack(mk).astype(ml_dtypes.bfloat16)
        m = dict(shared)
        m.update(x_full=x_full, x_light=x_light, p_in=np.ascontiguousarray(p_in), gsets=np.stack(gs), cfg=cfg, rope=rope, masks=masks)
        in_maps.append(m)
    nc, sc = build_program(T)
    res = run_bass_kernel_spmd(nc, in_maps, core_ids=list(range(NCORES)))
    y_prompt = np.zeros(xp.shape, np.float32); y_sample = np.zeros(xs.shape, np.float32)
    for c in range(NCORES):
        y = res.results[c]["y_out"]
        sq, q = c // 4, c % 4
        t0 = q * 2 * S
        for k in range(4):
            y_prompt[4 * c + k] = y[k]
        y_sample[sq, t0:t0 + S] = y[4]
        y_sample[sq, t0 + S:t0 + 2 * S] = y[5]
    return (y_prompt, y_sample)
```
